# Optimizing a Trainium2 kernel written in Bass

```python
import jax, jax.numpy as jnp
from jax import lax
import numpy as np

D_MODEL = 2048
BATCH = 1
SEQ = 8192
DEPTH = 1
DEC_BATCH = 16
DEC_SEQ = 16
PAST_LEN = 2048

CHUNK = 64
EPS = 1e-6
MLA_HEADS = 16
MLA_Q_RANK = 512
MLA_KV_RANK = 512
MLA_NOPE = 128
MLA_ROPE = 64
MLA_V = 128
MLA_QK_DIM = MLA_NOPE + MLA_ROPE
MLA_SCALE = MLA_QK_DIM ** -0.5
ROPE_THETA = 10000.0
Q_BLOCK = 128
GLA_HEADS = 4
GLA_DK = 256
GLA_DV = 512
GLA_GATE_RANK = 16
GLA_TAU = 16.0
GLA_BLOCK = 16
D_FF = 4 * D_MODEL
D_IN = (MLA_Q_RANK + MLA_KV_RANK + MLA_ROPE + 2 * GLA_HEADS * GLA_DK + 2 * GLA_HEADS * GLA_DV
        + GLA_GATE_RANK + 2 * D_MODEL)

kernel_name = "hybrid_mla_gla_streaming_step"


def rmsnorm(x, g):
    xf = x.astype(jnp.float32)
    inv = lax.rsqrt(jnp.mean(xf * xf, axis=-1, keepdims=True) + EPS)
    return (xf * inv * g.astype(jnp.float32)).astype(x.dtype)


def rope(x, pos):
    half = MLA_ROPE // 2
    freqs = jnp.power(ROPE_THETA, -jnp.arange(half, dtype=jnp.float32) / half)
    ang = pos[:, None] * freqs[None, :]
    shape = (ang.shape[0],) + (1,) * (x.ndim - 3) + (half,)
    cos = jnp.cos(ang).reshape(shape)
    sin = jnp.sin(ang).reshape(shape)
    xf = x.astype(jnp.float32)
    x1, x2 = xf[..., :half], xf[..., half:]
    return jnp.concatenate([x1 * cos - x2 * sin, x2 * cos + x1 * sin], axis=-1).astype(x.dtype)


def _front(x, pos, norm_g, w_in, q_norm_g, w_uq, kv_norm_g, gq_n, gq_r, gk_n, gk_r, w_a2, b_a):
    B, T, _ = x.shape
    f32 = jnp.float32
    h = rmsnorm(x, norm_g)
    z = h @ w_in
    gk_w, gv_w = GLA_HEADS * GLA_DK, GLA_HEADS * GLA_DV
    points = [int(p) for p in np.cumsum([MLA_Q_RANK, MLA_KV_RANK, MLA_ROPE, gk_w, gk_w, gv_w,
                                         GLA_GATE_RANK, gv_w, D_MODEL])]
    (q_lat, kv_lat, k_rope_raw, g_q, g_k, g_v, a_lr, out_gate, gate_mla, gate_gla) = jnp.split(z, points, axis=-1)
    q = jnp.einsum('btc,chd->bthd', rmsnorm(q_lat, q_norm_g), w_uq)
    q_nope = q[..., :MLA_NOPE].astype(f32)
    q_rope = rope(q[..., MLA_NOPE:], pos).astype(f32)
    ss = jnp.sum(q_nope * q_nope, -1, keepdims=True) + jnp.sum(q_rope * q_rope, -1, keepdims=True)
    inv_q = lax.rsqrt(ss / MLA_QK_DIM + EPS)
    qn = (q_nope * inv_q * (gq_n * gk_n).astype(f32) * MLA_SCALE).astype(x.dtype)
    qr = (q_rope * inv_q * jnp.tile((gq_r * gk_r).astype(f32), 2) * MLA_SCALE).astype(x.dtype)
    ckv = rmsnorm(kv_lat, kv_norm_g)
    krope = rope(k_rope_raw, pos)
    gla_q = g_q.reshape(B, T, GLA_HEADS, GLA_DK) * (GLA_DK ** -0.5)
    gla_k = g_k.reshape(B, T, GLA_HEADS, GLA_DK)
    gla_v = g_v.reshape(B, T, GLA_HEADS, GLA_DV)
    log_a = (jax.nn.log_sigmoid((a_lr @ w_a2 + b_a).astype(f32)) / GLA_TAU).reshape(B, T, GLA_HEADS, GLA_DK)
    return (qn, qr, ckv, krope, gla_q, gla_k, gla_v, log_a, out_gate, gate_mla, gate_gla)


def _mla_keys(ckv, krope, w_ukv):
    kv = jnp.einsum('bsc,chd->bshd', ckv, w_ukv)
    kn, v = kv[..., :MLA_NOPE], kv[..., MLA_NOPE:]
    knf = kn.astype(jnp.float32)
    krf = krope.astype(jnp.float32)
    ss = jnp.sum(knf * knf, -1) + jnp.sum(krf * krf, -1)[..., None]
    inv_k = lax.rsqrt(ss / MLA_QK_DIM + EPS)
    return kn, v, jnp.swapaxes(inv_k, 1, 2)


def _attend(qn, qr, kn, kr, v, inv_k, mask):
    s = (jnp.einsum('bqhd,bkhd->bhqk', qn, kn, preferred_element_type=jnp.float32)
         + jnp.einsum('bqhr,bkr->bhqk', qr, kr, preferred_element_type=jnp.float32))
    s = s * inv_k[:, :, None, :].astype(jnp.float32)
    if mask is not None:
        s = jnp.where(mask, s, -jnp.inf)
    p = jax.nn.softmax(s, axis=-1).astype(v.dtype)
    return jnp.einsum('bhqk,bkhd->bqhd', p, v)


def _mla_prompt_attention(qn, qr, kn, kr, v, inv_k):
    B, T, H, _ = qn.shape
    key_chunk = jnp.arange(T) // CHUNK

    def block(i):
        q0 = i * Q_BLOCK
        qn_b = lax.dynamic_slice_in_dim(qn, q0, Q_BLOCK, axis=1)
        qr_b = lax.dynamic_slice_in_dim(qr, q0, Q_BLOCK, axis=1)
        q_chunk = (q0 + jnp.arange(Q_BLOCK)) // CHUNK
        mask = key_chunk[None, :] <= q_chunk[:, None]
        return _attend(qn_b, qr_b, kn, kr, v, inv_k, mask)

    out = lax.map(block, jnp.arange(T // Q_BLOCK))
    return jnp.moveaxis(out, 0, 1).reshape(B, T, H * MLA_V)


def _gla_chunked(q, k, v, log_a, s0):
    B, T, H, DK = q.shape
    DV = v.shape[-1]
    L = GLA_BLOCK
    n = -(-T // L)
    pad = n * L - T
    f32 = jnp.float32

    def prep(a):
        a = jnp.pad(a.astype(f32), ((0, 0), (0, pad), (0, 0), (0, 0)))
        return a.reshape(B, n, L, H, a.shape[-1])

    qb, kb, vb, lab = prep(q), prep(k), prep(v), prep(log_a)
    b = jnp.cumsum(lab, axis=2)
    q_t = qb * jnp.exp(b)
    k_t = kb * jnp.exp(-b)
    b_end = b[:, :, -1]
    k_end = kb * jnp.exp(b_end[:, :, None] - b)
    tril = jnp.tril(jnp.ones((L, L), dtype=bool))
    A = jnp.where(tril, jnp.einsum('bnqhd,bnkhd->bnhqk', q_t, k_t), 0.0)
    o_intra = jnp.einsum('bnhqk,bnkhe->bnqhe', A, vb)

    def step(S, xs):
        q_n, k_n, v_n, be_n = xs
        o = jnp.einsum('blhd,bhde->blhe', q_n, S)
        S = S * jnp.exp(be_n)[..., None] + jnp.einsum('blhd,blhe->bhde', k_n, v_n)
        return S, o

    xs = (jnp.moveaxis(q_t, 1, 0), jnp.moveaxis(k_end, 1, 0), jnp.moveaxis(vb, 1, 0), jnp.moveaxis(b_end, 1, 0))
    s_fin, o_inter = lax.scan(step, s0.astype(f32), xs)
    o = o_intra + jnp.moveaxis(o_inter, 0, 1)
    return o.reshape(B, n * L, H, DV)[:, :T], s_fin


def _back(x, o_mla, o_gla, out_gate, gate_mla, gate_gla, gla_norm_g, w_o, norm_ffn_g, w_up, w_down):
    B, T, _ = x.shape
    o_gla = rmsnorm(o_gla, gla_norm_g).astype(x.dtype).reshape(B, T, GLA_HEADS * GLA_DV) * jax.nn.silu(out_gate)
    mixed = jax.nn.sigmoid(gate_mla) * o_mla + jax.nn.sigmoid(gate_gla) * o_gla
    x = x + mixed @ w_o
    h = rmsnorm(x, norm_ffn_g)
    return x + jnp.square(jax.nn.relu(h @ w_up)) @ w_down


def setup_inputs(seed: int = 0) -> dict:
    key = jax.random.key(seed)
    ks = jax.random.split(key, 22)
    f32 = jnp.float32

    def nrm(k, shape, scale=1.0):
        return jax.random.normal(k, shape, f32) * scale

    def gain(k, n):
        return 1.0 + 0.01 * jax.random.normal(k, (DEPTH, n), f32)

    return {
        "x_prompt": nrm(ks[0], (BATCH, SEQ, D_MODEL)),
        "x_sample": nrm(ks[1], (DEC_BATCH, DEC_SEQ, D_MODEL)),
        "cache_mla_ckv": nrm(ks[2], (DEPTH, DEC_BATCH, PAST_LEN, MLA_KV_RANK)),
        "cache_mla_krope": nrm(ks[3], (DEPTH, DEC_BATCH, PAST_LEN, MLA_ROPE)),
        "state_gla": nrm(ks[4], (DEPTH, DEC_BATCH, GLA_HEADS, GLA_DK, GLA_DV)),
        "norm_mix_g": gain(ks[5], D_MODEL),
        "w_in": nrm(ks[6], (DEPTH, D_MODEL, D_IN), D_MODEL ** -0.5),
        "mla_q_norm_g": gain(ks[7], MLA_Q_RANK),
        "mla_w_uq": nrm(ks[8], (DEPTH, MLA_Q_RANK, MLA_HEADS, MLA_QK_DIM), MLA_Q_RANK ** -0.5),
        "mla_kv_norm_g": gain(ks[9], MLA_KV_RANK),
        "mla_w_ukv": nrm(ks[10], (DEPTH, MLA_KV_RANK, MLA_HEADS, MLA_NOPE + MLA_V), MLA_KV_RANK ** -0.5),
        "mla_q_gain_nope": gain(ks[11], MLA_NOPE),
        "mla_q_gain_rope": gain(ks[12], MLA_ROPE // 2),
        "mla_k_gain_nope": gain(ks[13], MLA_NOPE),
        "mla_k_gain_rope": gain(ks[14], MLA_ROPE // 2),
        "gla_w_a2": nrm(ks[15], (DEPTH, GLA_GATE_RANK, GLA_HEADS * GLA_DK), GLA_GATE_RANK ** -0.5),
        "gla_b_a": nrm(ks[16], (DEPTH, GLA_HEADS * GLA_DK), 0.1),
        "gla_norm_g": gain(ks[17], GLA_DV),
        "w_o": nrm(ks[18], (DEPTH, D_MODEL, D_MODEL), D_MODEL ** -0.5),
        "norm_ffn_g": gain(ks[19], D_MODEL),
        "ffn_w_up": nrm(ks[20], (DEPTH, D_MODEL, D_FF), D_MODEL ** -0.5),
        "ffn_w_down": nrm(ks[21], (DEPTH, D_FF, D_MODEL), D_FF ** -0.5),
    }


def reference(x_prompt, x_sample, cache_mla_ckv, cache_mla_krope, state_gla,
              norm_mix_g, w_in, mla_q_norm_g, mla_w_uq, mla_kv_norm_g, mla_w_ukv,
              mla_q_gain_nope, mla_q_gain_rope, mla_k_gain_nope, mla_k_gain_rope,
              gla_w_a2, gla_b_a, gla_norm_g, w_o, norm_ffn_g, ffn_w_up, ffn_w_down):
    past_len = cache_mla_ckv.shape[2]
    pos_p = jnp.arange(x_prompt.shape[1], dtype=jnp.float32)
    pos_s = past_len + jnp.arange(x_sample.shape[1], dtype=jnp.float32)
    xp, xs = x_prompt, x_sample
    ckv_p, kr_p, st_p, ckv_s, kr_s, st_s = [], [], [], [], [], []
    for l in range(DEPTH):
        front_w = (norm_mix_g[l], w_in[l], mla_q_norm_g[l], mla_w_uq[l], mla_kv_norm_g[l],
                   mla_q_gain_nope[l], mla_q_gain_rope[l], mla_k_gain_nope[l], mla_k_gain_rope[l],
                   gla_w_a2[l], gla_b_a[l])
        back_w = (gla_norm_g[l], w_o[l], norm_ffn_g[l], ffn_w_up[l], ffn_w_down[l])

        qn, qr, ckv, kr, gq, gk, gv, la, og, gm, gg = _front(xp, pos_p, *front_w)
        kn, v, inv_k = _mla_keys(ckv, kr, mla_w_ukv[l])
        o_mla = _mla_prompt_attention(qn, qr, kn, kr, v, inv_k)
        s0 = jnp.zeros((xp.shape[0], GLA_HEADS, GLA_DK, GLA_DV), jnp.float32)
        o_gla, s_new = _gla_chunked(gq, gk, gv, la, s0)
        xp_next = _back(xp, o_mla, o_gla, og, gm, gg, *back_w)
        ckv_p.append(ckv)
        kr_p.append(kr)
        st_p.append(s_new.astype(xp.dtype))
        xp = xp_next

        qn, qr, ckv, kr, gq, gk, gv, la, og, gm, gg = _front(xs, pos_s, *front_w)
        ckv_all = jnp.concatenate([cache_mla_ckv[l], ckv], axis=1)
        kr_all = jnp.concatenate([cache_mla_krope[l], kr], axis=1)
        kn, v, inv_k = _mla_keys(ckv_all, kr_all, mla_w_ukv[l])
        o_mla = _attend(qn, qr, kn, kr_all, v, inv_k, None).reshape(xs.shape[0], xs.shape[1], MLA_HEADS * MLA_V)
        o_gla, s_new = _gla_chunked(gq, gk, gv, la, state_gla[l])
        xs_next = _back(xs, o_mla, o_gla, og, gm, gg, *back_w)
        ckv_s.append(ckv)
        kr_s.append(kr)
        st_s.append(s_new.astype(state_gla.dtype))
        xs = xs_next
    return (xp, xs, jnp.stack(ckv_p), jnp.stack(kr_p), jnp.stack(st_p),
            jnp.stack(ckv_s), jnp.stack(kr_s), jnp.stack(st_s))
```

```python
import numpy as np
import concourse.bass as bass
import concourse.mybir as mybir
from concourse.bass_utils import run_bass_kernel_spmd

F32 = mybir.dt.float32
BF16 = mybir.dt.bfloat16
AF = mybir.ActivationFunctionType
ALU = mybir.AluOpType
AX = mybir.AxisListType

NCORES = 8
D = 2048
T = 8192
NT = 64
DIN = 11344
EPS = 1e-6
C_QLAT, C_KV, C_KR, C_GQ, C_GK, C_GV, C_ALR, C_OG, C_GM, C_GG = 0, 512, 1024, 1088, 2112, 3136, 5184, 5200, 7248, 9296
NEG = -30000.0


class Ins:
    __slots__ = ("eng", "fn", "deps", "signal", "val", "semkey", "inc")

    def __init__(self, eng, fn, semkey, inc):
        self.eng, self.fn, self.semkey, self.inc = eng, fn, semkey, inc
        self.deps = []
        self.signal = False
        self.val = None


class Sched:
    ENGS = ("pe", "dve", "act", "pool", "sp")

    def __init__(self):
        self.ins = {e: [] for e in self.ENGS}
        self.lastw = {}
        self.readers = {}
        self.chan_last = {}
        self.all_dma = []

    def op(self, eng, fn, r=(), w=(), chan=None):
        if chan is not None:
            I = Ins(eng, fn, ("ch", chan), 16)
            I.signal = True
        else:
            I = Ins(eng, fn, eng, 1)
        deps = []
        for k in r:
            x = self.lastw.get(k)
            if x is not None:
                deps.append(x)
        for k in w:
            x = self.lastw.get(k)
            if x is not None:
                deps.append(x)
            deps.extend(self.readers.get(k, ()))
        if chan is not None:
            x = self.chan_last.get(chan)
            if x is not None:
                deps.append(x)
            self.chan_last[chan] = I
            self.all_dma.append(I)
        seen = set()
        for d in deps:
            if d is I or id(d) in seen:
                continue
            seen.add(id(d))
            if d.eng == "pe" and eng == "pe" and d.semkey == "pe":
                continue
            d.signal = True
            I.deps.append(d)
        for k in w:
            self.lastw[k] = I
            self.readers[k] = []
        for k in r:
            if k not in w:
                self.readers.setdefault(k, []).append(I)
        self.ins[eng].append(I)
        return I

    def barrier(self, nopfn):
        lasts = []
        for e in self.ENGS:
            if e != "sp" and self.ins[e]:
                for I in reversed(self.ins[e]):
                    if I.semkey == e:
                        lasts.append(I)
                        break
        lasts.extend(self.chan_last.values())
        for e in self.ENGS:
            I = Ins(e, nopfn[e], e, 1)
            for d in lasts:
                if d.eng == e and d.semkey == e and e == "pe":
                    continue
                d.signal = True
                I.deps.append(d)
            self.ins[e].append(I)
        self.lastw = {}
        self.readers = {}

    def assign(self):
        cnt = {}
        for e in self.ENGS:
            for I in self.ins[e]:
                if I.signal:
                    cnt[I.semkey] = cnt.get(I.semkey, 0) + I.inc
                    I.val = cnt[I.semkey]
        self.finals = dict((k, v) for k, v in cnt.items() if isinstance(k, tuple))
        self.cnt = cnt

    def emit_engine(self, e, eng, sems):
        seen = {}
        for I in self.ins[e]:
            need = {}
            for d in I.deps:
                if d.val > need.get(d.semkey, 0):
                    need[d.semkey] = d.val
            for k, v in need.items():
                if seen.get(k, 0) < v:
                    eng.wait_ge(sems[k], v)
                    seen[k] = v
            bi = I.fn(eng)
            if I.signal:
                bi.then_inc(sems[I.semkey], I.inc)
        if e == "sp":
            for k, v in self.finals.items():
                if seen.get(k, 0) < v:
                    eng.wait_ge(sems[k], v)


class Ctx:
    pass


def _bcast_rows(ap_row, nparts):
    return ap_row.partition_broadcast(nparts)


def build_program(stage=3, dbg=False):
    import os
    KR = int(os.environ.get("KR", "8"))
    KSKIP = os.environ.get("KSKIP", "nodbg")
    from contextlib import ExitStack
    nc = bass.Bass("TRN2", target_bir_lowering=False)
    S = Sched()
    es = ExitStack()
    C = Ctx()

    def din(name, shape, dt=F32):
        return nc.dram_tensor(name, list(shape), dt, kind="ExternalInput").ap()

    def dout(name, shape, dt=F32):
        return nc.dram_tensor(name, list(shape), dt, kind="ExternalOutput").ap()

    def dscr(name, shape, dt):
        return nc.dram_tensor(name, list(shape), dt, kind="ExternalOutput").ap()

    C.cur = es
    C.uid = 0
    C.scopes = []

    def new_scope():
        sc = ExitStack()
        C.scopes.append(sc)
        C.cur = sc
        return sc

    def sb(name, shape, dt):
        C.uid += 1
        return C.cur.enter_context(nc.sbuf_tensor("%s_%d" % (name, C.uid), list(shape), dt))

    def ps(name, shape, dt):
        return es.enter_context(nc.psum_tensor(name, list(shape), dt))

    xs = din("xs", [NT * 128, D])
    ropet = din("ropet", [NT * 128, 64])
    kbias = din("kbias", [128, NT + 1])
    consts = din("consts", [128, 3 * 128])
    w_in = din("w_in", [D, DIN])
    w_uq = din("w_uq", [512, 16 * 192])
    w_ukv = din("w_ukv", [512, 16 * 256])
    w_a2 = din("w_a2", [16, 1024])
    b_a = din("b_a", [1, 1024])
    w_o = din("w_o", [D, D])
    w_up = din("w_up", [D, 4 * D])
    w_dn = din("w_dn", [4 * D, D])
    g_mix = din("g_mix", [1, D])
    g_ffn = din("g_ffn", [1, D])
    g_q = din("g_q", [1, 512])
    g_kv = din("g_kv", [1, 512])
    g_gla = din("g_gla", [1, 512])
    gq_n = din("gq_n", [1, 128])
    gk_n = din("gk_n", [1, 128])
    gq_r = din("gq_r", [1, 32])
    gk_r = din("gk_r", [1, 32])
    x_s = din("x_s", [32, D])
    ropes = din("ropes", [32, 64])
    cckv = din("cckv", [2, 2048, 512])
    ckr = din("ckr", [2, 2048, 64])
    st_in = din("st_in", [2, 4, 256, 512])

    y_p = dout("y_p", [1024, D])
    ckv_p = dout("ckv_p", [1024, 512])
    kr_p = dout("kr_p", [1024, 64])
    st_p = dout("st_p", [4, 256, 512])
    y_s = dout("y_s", [32, D])
    ckv_s = dout("ckv_s", [32, 512])
    kr_s = dout("kr_s", [32, 64])
    st_s = dout("st_s", [2, 4, 256, 512])

    knT_scr = dscr("knT_scr", [16, 128, T], BF16)
    krT_scr = dscr("krT_scr", [64, T], BF16)
    v_scr = dscr("v_scr", [16, 128, NT, 128], BF16)
    invk_scr = dscr("invk_scr", [128, NT, 16], F32)
    ogla_scr = dscr("ogla_scr", [8, 128, 16, 128], BF16)

    cst = sb("cst", [128, 384], F32)
    ident_f = cst[:, 0:128]
    tri_f = cst[:, 128:256]
    aft_f = cst[:, 256:384]
    ident_b = sb("ident_b", [128, 128], BF16)
    ones_b = sb("ones_b", [128, 128], BF16)
    ones_f = sb("ones_f", [128, 128], F32)
    gmixT = sb("gmixT", [128, 16], F32)
    gffnT = sb("gffnT", [128, 16], F32)
    gkv_bc = sb("gkv_bc", [128, 512], F32)
    ggla_bc = sb("ggla_bc", [128, 512], F32)
    wa2_f = sb("wa2_f", [17, 1024], F32)
    kb_sb = sb("kb_sb", [128, NT + 1], F32)

    def dma(q, out, in_, r, w, chan, slow=False):
        if any(str(k).startswith("scr_") for k in w):
            if "spq" in KSKIP:
                q = "sp"
            if "kvdma" in KSKIP or any(tok and tok in str(chan) for tok in KSKIP.split(",")):
                return None
        return S.op(q, lambda e, o=out, i=in_: e.dma_start(out=o, in_=i, allow_slow_non_contiguous=slow), r=r, w=w, chan=chan)

    dma("sp", cst[:, :], consts, [], ["cst"], "c0")
    dma("sp", gmixT[:, :], g_mix.rearrange("o (k p) -> p (o k)", p=128), [], ["gmixT"], "c0", slow=True)
    dma("sp", gffnT[:, :], g_ffn.rearrange("o (k p) -> p (o k)", p=128), [], ["gffnT"], "c0", slow=True)
    dma("sp", gkv_bc[:, :], g_kv.partition_broadcast(128), [], ["gkv_bc"], "c0")
    dma("sp", ggla_bc[:, :], g_gla.partition_broadcast(128), [], ["ggla_bc"], "c0")
    dma("sp", wa2_f[0:16, :], w_a2, [], ["wa2_f"], "c0")
    dma("sp", wa2_f[16:17, :], b_a, [], ["wa2_f"], "c0")
    dma("sp", kb_sb[:, :], kbias, [], ["kb_sb"], "c0")
    S.op("dve", lambda e: e.tensor_copy(ident_b[:, :], ident_f), r=["cst"], w=["ident_b"])
    S.op("pool", lambda e: e.memset(ones_b[:, :], 1.0), w=["ones_b"])
    S.op("pool", lambda e: e.memset(ones_f[:, :], 1.0), w=["ones_f"])

    banks = [ps("bank%d" % i, [128, 512], F32) for i in range(8)]
    C.bank_rr = 0
    bar_t = sb("bar_t", [128, 8], F32)
    NOPS = {e_: (lambda e: e.nop()) for e_ in ("pe", "dve", "act", "pool", "sp")}

    def nextbank():
        b = C.bank_rr
        C.bank_rr = (C.bank_rr + 1) % 8
        return b

    def alloc_wbufs(nbuf=5, size=4096):
        C.wbf = [sb("wbf%d" % i, [128, size], BF16) for i in range(nbuf)]
        C.nwb, C.wsize = nbuf, size
        C.wbf_rr = 0
    C.wbf_rr = 0

    def load_slab(src_ap, kcs, ncols):
        wbf = C.wbf
        bi = C.wbf_rr
        C.wbf_rr = (bi + 1) % C.nwb
        n = kcs * ncols
        assert n <= C.wsize
        dv = wbf[bi][:, 0:n].rearrange("p (k c) -> p k c", k=kcs)
        srcv = src_ap.rearrange("(k p) c -> p k c", p=128)
        dma("pool", dv, srcv, [], [("wbf", bi)], ("wbf", bi))
        return dv, ("wbf", bi)

    xt = [sb("xt%d" % i, [128, D], F32) for i in range(1)]
    junk = sb("junk", [128, 64], BF16)
    hb = [sb("hb%d" % i, [128, D], BF16) for i in range(1)]
    hT = sb("hT", [128, 16, 1056], BF16)
    stat = sb("stat", [128, 64], F32)
    C.stat_rr = 0

    def statcol():
        c = C.stat_rr
        C.stat_rr = (c + 1) % 64
        return c

    def norm_transpose(x_rows_ap, ntok, gT, col0, hT_key, xkeep=None):
        i = 0
        dma("sp", xt[i][0:ntok, :], x_rows_ap, [], [("xt", i)], ("xt", i))
        c = statcol()
        ss = stat[0:ntok, c:c + 1]
        S.op("act", lambda e: e.activation(hb[i][0:ntok, :], xt[i][0:ntok, :], AF.Square, accum_out=ss),
             r=[("xt", i)], w=[("stat", c), ("hb", i)])
        S.op("act", lambda e: e.activation(ss, ss, AF.Ln, bias=EPS, scale=1.0 / D), r=[("stat", c)], w=[("stat", c)])
        S.op("act", lambda e: e.activation(ss, ss, AF.Exp, scale=-0.5), r=[("stat", c)], w=[("stat", c)])
        S.op("dve", lambda e: e.tensor_scalar(hb[i][0:ntok, :], xt[i][0:ntok, :], ss, None, ALU.mult),
             r=[("xt", i), ("stat", c)], w=[("hb", i)])
        for g4 in range(4):
            b = rbank()
            pv = banks[b][:, :].bitcast(BF16)
            for j in range(4):
                kc = g4 * 4 + j
                S.op("pe", lambda e, o=pv[:, j * 128:j * 128 + ntok], a=hb[i][0:ntok, kc * 128:(kc + 1) * 128]:
                     e.transpose(o, a, ident_b[0:ntok, 0:ntok]),
                     r=[("hb", i), "ident_b"], w=[("bank", b)])
            for j in range(4):
                kc = g4 * 4 + j
                if g4 % 2 == 0:
                    S.op("act", lambda e, o=hT[:, kc, col0:col0 + ntok], a=pv[:, j * 128:j * 128 + ntok], sc=gT[:, kc:kc + 1]:
                         e.activation(o, a, AF.Copy, scale=sc),
                         r=[("bank", b), "gmixT", "gffnT"], w=[hT_key])
                else:
                    S.op("dve", lambda e, o=hT[:, kc, col0:col0 + ntok], a=pv[:, j * 128:j * 128 + ntok], sc=gT[:, kc:kc + 1]:
                         e.tensor_scalar(o, a, sc, None, ALU.mult),
                         r=[("bank", b), "gmixT", "gffnT"], w=[hT_key])

    C.RB = 5

    def rbank():
        b = C.bank_rr
        C.bank_rr = (C.bank_rr + 1) % C.RB
        return b

    def proj_tm(wv, wkey, kcs, ncols, src, skey_fn, tiles, evac, after_tile=None):
        for ti, (col0, ntok, tag) in enumerate(tiles):
            b = rbank()
            pt = banks[b][0:ntok, 0:ncols]
            for kc in range(kcs):
                S.op("pe", lambda e, o=pt, l=src[:, kc, col0:col0 + ntok], rr=wv[:, kc, :], st=(kc == 0), sp=(kc == kcs - 1):
                     e.matmul(o, l, rr, start=st, stop=sp), r=[wkey, skey_fn(ti)], w=[("bank", b)])
            evac(ti, tag, pt, ("bank", b))
            if after_tile is not None:
                after_tile(ti)

    def proj_fm(wv, wkey, kcs, c0, m, src, skeys, n0, n, evac):
        b = rbank()
        pt = banks[b][0:m, 0:n]
        for kc in range(kcs):
            S.op("pe", lambda e, o=pt, l=wv[:, kc, c0:c0 + m], rr=src[:, kc, n0:n0 + n], st=(kc == 0), sp=(kc == kcs - 1):
                 e.matmul(o, l, rr, start=st, stop=sp), r=[wkey] + list(skeys), w=[("bank", b)])
        evac(pt, ("bank", b))

    def act_rstd(out_ap, ss_ap, n, rkeys, wkeys):
        S.op("act", lambda e: e.activation(out_ap, ss_ap, AF.Ln, bias=EPS, scale=1.0 / n), r=rkeys, w=wkeys)
        S.op("act", lambda e: e.activation(out_ap, out_ap, AF.Exp, scale=-0.5), r=wkeys, w=wkeys)

    es1 = new_scope()
    alloc_wbufs(3, 8192)
    ckvr = sb("ckvr", [128, 8, 512], F32)
    ckvb = [sb("ckvb%d" % i, [128, 512], BF16) for i in range(2)]
    ckvT = sb("ckvT", [128, 4, 1024], BF16)
    krw = sb("krw", [128, 8, 80], F32)
    krr = sb("krr", [128, 8, 64], F32)
    krtmp = sb("krtmp", [128, 4, 32], F32)
    krb = [sb("krb%d" % i, [128, 64], BF16) for i in range(2)]
    krT = sb("krT", [64, 1024], BF16)
    krss = sb("krss", [128, 8], F32)
    ropc = sb("ropc", [128, 8, 64], F32)
    alrT = sb("alrT", [17, 1024], F32)
    lnl2 = [hb[0][:, :].bitcast(F32), xt[0][:, 0:1024]]
    Ebuf = sb("Ebuf", [128, 8, 1024], BF16)
    Dt = sb("Dt", [128, 8, 8], F32)
    kend = sb("kend", [128, 8, 256], BF16)
    vh = sb("vh", [128, 8, 512], BF16)
    Sst = sb("Sst", [128, 8, 512], F32)
    Sbf = sb("Sbf", [128, 2, 512], BF16)
    knT = [sb("knT%d" % i, [128, 1024], BF16) for i in range(2)]
    sqb = [sb("sqb%d" % i, [128, 512], BF16) for i in range(3)]
    vt = vh
    invk = sb("invk", [128, 128], F32)
    qkT = sb("qkT", [128, 4, 128], BF16)
    eqk = sb("eqk", [128, 4, 128], F32)
    ATb = sb("ATb", [128, 128], BF16)
    onb = sb("onb", [128, 512], BF16)
    ogT = sb("ogT", [128, 16, 128], BF16)
    cumT = xt[0][:, 1024:2048].rearrange("p (a b) -> p a b", a=8)

    def tile_keys(prefix, n=8):
        return [(prefix, i) for i in range(n)]

    def rope_tm(dst, src, cs, ntok, rk, wk):
        x1, x2 = src[:, 0:32], src[:, 32:64]
        co, si = cs[:, 0:32], cs[:, 32:64]
        t = krtmp
        S.op("dve", lambda e: e.tensor_tensor(t[0:ntok, 0, :], x1, co, ALU.mult), r=rk, w=["krtmp"])
        S.op("dve", lambda e: e.tensor_tensor(t[0:ntok, 1, :], x2, si, ALU.mult), r=rk, w=["krtmp"])
        S.op("dve", lambda e: e.tensor_tensor(t[0:ntok, 2, :], x2, co, ALU.mult), r=rk, w=["krtmp"])
        S.op("dve", lambda e: e.tensor_tensor(t[0:ntok, 3, :], x1, si, ALU.mult), r=rk, w=["krtmp"])
        S.op("dve", lambda e: e.tensor_tensor(dst[:, 0:32], t[0:ntok, 0, :], t[0:ntok, 1, :], ALU.subtract), r=["krtmp"], w=wk)
        S.op("dve", lambda e: e.tensor_tensor(dst[:, 32:64], t[0:ntok, 2, :], t[0:ntok, 3, :], ALU.add), r=["krtmp"], w=wk)

    def kv_finish(tiles, dst, kt0, fill=None):
        fill = fill if fill is not None else []

        def pumpf():
            if fill:
                fill.pop(0)()
        n = len(tiles)
        ncols = n * 128
        for idx, (i, ntok) in enumerate(tiles):
            j = idx % 2
            if ntok < 128:
                S.op("dve", lambda e, j=j: e.memset(ckvb[j][:, :], 0.0), w=[("ckvb", j)])
                S.op("dve", lambda e, j=j: e.memset(krb[j][:, :], 0.0), w=[("krb", j)])
            S.op("act", lambda e, j=j, i=i, ntok=ntok: e.activation(ckvb[j][0:ntok, :], ckvr[0:ntok, i, :], AF.Copy),
                 r=[("ckvr", i)], w=[("ckvb", j)])
            S.op("dve", lambda e, j=j, i=i, ntok=ntok: e.tensor_copy(krb[j][0:ntok, :], krr[0:ntok, i, :]),
                 r=[("krr", i)], w=[("krb", j)])
            S.op("act", lambda e, i=i, ntok=ntok: e.activation(junk[0:ntok, 0:64], krr[0:ntok, i, :], AF.Square,
                                                              accum_out=krss[0:ntok, i:i + 1]),
                 r=[("krr", i)], w=[("krss", i), "junk"])
            b = rbank()
            pv = banks[b][:, :].bitcast(BF16)
            for kc in range(4):
                S.op("pe", lambda e, kc=kc, j=j, pv=pv: e.transpose(pv[:, kc * 128:(kc + 1) * 128], ckvb[j][:, kc * 128:(kc + 1) * 128], ident_b[:, :]),
                     r=[("ckvb", j), "ident_b"], w=[("bank", b)])
            b2 = rbank()
            pv2 = banks[b2][:, :].bitcast(BF16)
            S.op("pe", lambda e, j=j, pv2=pv2: e.transpose(pv2[0:64, 0:128], krb[j][:, :], ident_b[:, :]),
                 r=[("krb", j), "ident_b"], w=[("bank", b2)])
            if "skipB" not in KSKIP:
                S.op("act", lambda e, pv=pv, idx=idx: e.activation(
                    ckvT[:, :, idx * 128:(idx + 1) * 128], pv[:, 0:512].rearrange("p (k c) -> p k c", k=4), AF.Copy),
                    r=[("bank", b)], w=[("ckvT", idx)])
            S.op("dve", lambda e, pv2=pv2, idx=idx: e.tensor_copy(krT[:, idx * 128:(idx + 1) * 128], pv2[0:64, 0:128]),
                 r=[("bank", b2)], w=[("krT", idx)])
        if "var2" in KSKIP:
            dma("act", y_p[0:64, 0:ncols // 2].bitcast(BF16), krT[:, 0:ncols], [("krT", i) for i in range(n)], [("scr_krT")], "krT_o")
        elif "var1" in KSKIP:
            dma("act", dst["krT"][:, kt0 * 128:kt0 * 128 + ncols], hb[0][0:64, 0:ncols], [("hb", 0)], [("scr_krT")], "krT_o")
        else:
            dma("act", dst["krT"][:, kt0 * 128:kt0 * 128 + ncols], krT[:, 0:ncols], [("krT", i) for i in range(n)], [("scr_krT")], "krT_o")
        if "cut1" in KSKIP:
            return
        halves = [(c0, min(512, ncols - c0)) for c0 in range(0, ncols, 512)]
        pend_ssk = []
        for sl in range(4):
            wv, wkey = load_slab(w_ukv[:, sl * 1024:(sl + 1) * 1024], 4, 1024)
            for hh in range(4):
                h = sl * 4 + hh
                kb = h % 2
                if hh % 2 == 0:
                    pumpf()
                for (c0, cn) in halves:
                    b = rbank()
                    pt = banks[b][:, 0:cn]
                    for kc in range(4):
                        S.op("pe", lambda e, pt=pt, kc=kc, hh=hh, c0=c0, cn=cn, wv=wv: e.matmul(
                            pt, wv[:, kc, hh * 256:hh * 256 + 128], ckvT[:, kc, c0:c0 + cn], start=(kc == 0), stop=(kc == 3)),
                            r=[wkey] + [("ckvT", t) for t in range(c0 // 128, (c0 + cn) // 128)], w=[("bank", b)])
                    S.op("dve", lambda e, pt=pt, kb=kb, c0=c0, cn=cn: e.tensor_copy(knT[kb][:, c0:c0 + cn], pt),
                         r=[("bank", b)], w=[("knT", kb)])
                    sj = C.sq_rr = (getattr(C, "sq_rr", 0) + 1) % 3
                    S.op("act", lambda e, kb=kb, sj=sj, c0=c0, cn=cn: e.activation(sqb[sj][:, 0:cn], knT[kb][:, c0:c0 + cn], AF.Square),
                         r=[("knT", kb)], w=[("sqb", sj)])
                    def ssk_mm(sj=sj, c0=c0, cn=cn, h=h):
                        for t in range(cn // 128):
                            idx = c0 // 128 + t
                            S.op("pe", lambda e, t=t, idx=idx: e.matmul(
                                banks[7][:, idx * 16 + h:idx * 16 + h + 1], sqb[sj][:, t * 128:(t + 1) * 128], ones_b[:, 0:1],
                                start=True, stop=True), r=[("sqb", sj), "ones_b"], w=["ssk"])
                    if pend_ssk:
                        pend_ssk.pop(0)()
                    pend_ssk.append(ssk_mm)
                dma("act", dst["knT"][h, :, kt0 * 128:kt0 * 128 + ncols], knT[kb][:, 0:ncols], [("knT", kb)], ["scr_knT"], ("knT_o", kb))
            for idx in range(n if "skipD" not in KSKIP else 0):
                b = rbank()
                pt = banks[b][:, 0:512]
                for kc in range(4):
                    rhs = wv[:, kc, :].rearrange("p (h c) -> p h c", h=4)[:, :, 128:256]
                    S.op("pe", lambda e, pt=pt, kc=kc, idx=idx, rhs=rhs: e.matmul(
                        pt.rearrange("p (h c) -> p h c", h=4), ckvT[:, kc, idx * 128:(idx + 1) * 128], rhs, start=(kc == 0), stop=(kc == 3)),
                        r=[wkey, ("ckvT", idx)], w=[("bank", b)])
                eng = "act" if idx % 2 else "dve"
                if eng == "act":
                    S.op("act", lambda e, pt=pt, idx=idx: e.activation(vt[:, idx, :], pt, AF.Copy), r=[("bank", b)], w=["vh"])
                else:
                    S.op("dve", lambda e, pt=pt, idx=idx: e.tensor_copy(vt[:, idx, :], pt), r=[("bank", b)], w=["vh"])
            for hh in range(4):
                dma("act", dst["v"][sl * 4 + hh, :, kt0:kt0 + n, :], vt[:, 0:n, hh * 128:(hh + 1) * 128], ["vh"], ["scr_v"], ("vt_o", hh))
        while pend_ssk:
            pend_ssk.pop(0)()
        while fill:
            pumpf()
        for idx, (i, ntok) in enumerate(tiles):
            S.op("dve", lambda e, idx=idx, i=i: e.tensor_scalar(invk[:, idx * 16:(idx + 1) * 16], banks[7][:, idx * 16:(idx + 1) * 16],
                                                                 krss[:, i:i + 1], 1.0 / 192.0, ALU.add, ALU.mult),
                 r=["ssk", ("krss", i)], w=["invk"])
        act_rstd(invk[:, 0:n * 16], invk[:, 0:n * 16], 1.0, ["invk"], ["invk"])
        if "var5" in KSKIP:
            S.op("dve", lambda e: e.tensor_copy(stat[:, 0:16], invk[:, 0:16]), r=["invk"], w=[("stat", 0)])
        elif "var6" in KSKIP:
            S.op("pool", lambda e: e.tensor_copy(hb[0][:, 0:128].bitcast(F32), invk[:, 0:64]), r=["invk"], w=[("hb", 0)])
            dma("act", kr_p[0:128, :], hb[0][:, 0:128].bitcast(F32), [("hb", 0)], [], "zz")
        elif "var3" in KSKIP:
            dma("sp" if "var3sp" in KSKIP else "act", kr_p[0:128, :], invk[:, 0:64] if "var3hb" not in KSKIP else hb[0][:, 0:128].bitcast(F32), ["invk"], [], "zz")
        elif "var4" in KSKIP:
            dma("act", dst["invk"][:, kt0:kt0 + n, :], invk[:, 0:n * 16].rearrange("p (j h) -> p j h", h=16), ["invk"], [], "zz")
        else:
            dma("act", dst["invk"][:, kt0:kt0 + n, :], invk[:, 0:n * 16].rearrange("p (j h) -> p j h", h=16), ["invk"], ["scr_invk"], "invk_o")


    knT_s = dout("knT_s", [2, 16, 128, 17 * 128], BF16)
    krT_s = dout("krT_s", [2, 64, 17 * 128], BF16)
    v_s = dout("v_s", [2, 16, 128, 17, 128], BF16)
    invk_s = dout("invk_s", [2, 128, 17, 16], F32)
    ogla_s = dout("ogla_s", [2, 128, 16, 16], BF16)
    S.op("pool", lambda e: e.memset(alrT[:, :], 1.0), w=["alrT"])
    S.op("pool", lambda e: e.memset(krss[:, :], 1.0), w=[("krss", i) for i in range(8)])
    S.op("pool", lambda e: e.memset(Sst[:, :, :], 0.0), w=["Sst"])

    def evac_copy(eng, out, pt, bkey, wkeys):
        if eng == "act":
            S.op("act", lambda e: e.activation(out, pt, AF.Copy), r=[bkey], w=wkeys)
        else:
            S.op("dve", lambda e: e.tensor_copy(out, pt), r=[bkey], w=wkeys)

    def front_round(tiles, rope_src, own, outs, ogdst):
        ptiles = [(c0, nt, i) for (i, c0, nt) in tiles]
        hk = lambda ti: ("hT", tiles[ti][0])
        allhk = [("hT", i) for (i, _, _) in tiles]
        ntot = sum(nt for (_, _, nt) in tiles)
        cbase = tiles[0][1]
        for (i, c0, nt) in tiles:
            dma("sp", ropc[0:nt, i, :], rope_src(i, nt), [], [("ropc", i)], ("ropc", i % 2))
        wv, wkey = load_slab(w_in[:, C_ALR:C_ALR + 16], 16, 16)
        for n0 in range(0, ntot, 512):
            n = min(512, ntot - n0)
            proj_fm(wv, wkey, 16, 0, 16, hT, allhk, cbase + n0, n,
                    lambda pt, bk, n0=n0, n=n: evac_copy("act", alrT[0:16, n0:n0 + n], pt, bk, ["alrT"]))
        stages = []
        LK = [("hb", 0), ("xt", 0)]
        for ti, (i, c0, nt) in enumerate(tiles):
            a0 = c0 - cbase

            def stA(i=i, nt=nt, a0=a0):
                for half in range(2):
                    b = rbank()
                    pt = banks[b][0:nt, :]
                    S.op("pe", lambda e, pt=pt, half=half: e.matmul(pt, alrT[0:17, a0:a0 + nt], wa2_f[0:17, half * 512:(half + 1) * 512], start=True, stop=True),
                         r=["alrT", "wa2_f"], w=[("bank", b)])
                    S.op("act", lambda e, pt=pt, half=half: e.activation(lnl2[i % 2][0:nt, half * 512:(half + 1) * 512], pt, AF.Exp, scale=-1.0),
                         r=[("bank", b)], w=[LK[i % 2]])
                S.op("act", lambda e: e.activation(lnl2[i % 2][0:nt, :], lnl2[i % 2][0:nt, :], AF.Ln, bias=1.0), r=[LK[i % 2]], w=[LK[i % 2]])

            def stB(i=i, nt=nt):
                for half in range(2):
                    b = rbank()
                    pt = banks[b][0:nt, :]
                    S.op("pe", lambda e, pt=pt, half=half: e.matmul(pt, aft_f[0:nt, 0:nt], lnl2[i % 2][0:nt, half * 512:(half + 1) * 512], start=True, stop=True),
                         r=[LK[i % 2], "cst"], w=[("bank", b)])
                    S.op("act", lambda e, pt=pt, half=half: e.activation(Ebuf[0:nt, i, half * 512:(half + 1) * 512], pt, AF.Exp, scale=-1.0 / 16),
                         r=[("bank", b)], w=[("Ebuf", i)])

            def stC(i=i, nt=nt, ti=ti):
                db = 5 + (ti % 2)
                for ch in range(8):
                    S.op("pe", lambda e, ch=ch: e.matmul(banks[db][:, ch:ch + 1], lnl2[i % 2][0:nt, ch * 128:(ch + 1) * 128], ones_f[0:nt, 0:1], start=True, stop=True),
                         r=[LK[i % 2], "ones_f"], w=[("bank", db)])
                S.op("act", lambda e: e.activation(Dt[:, i, :], banks[db][:, 0:8], AF.Exp, scale=-1.0 / 16), r=[("bank", db)], w=[("Dt", i)])
                if own == i:
                    for ch in range(8):
                        b = rbank()
                        pt = banks[b][:, 0:nt]
                        S.op("pe", lambda e, pt=pt, ch=ch: e.matmul(pt, lnl2[i % 2][0:nt, ch * 128:(ch + 1) * 128], tri_f[0:nt, 0:nt], start=True, stop=True),
                             r=[LK[i % 2], "cst"], w=[("bank", b)])
                        S.op("act", lambda e, pt=pt, ch=ch: e.activation(cumT[:, ch, 0:nt], pt, AF.Copy), r=[("bank", b)], w=[("xt", 0)])
            stages += [stA, stB, stC]
        if "gla" in KSKIP:
            stages = []

        def pump(n=1):
            for _ in range(n):
                if stages:
                    stages.pop(0)()
        wv, wkey = load_slab(w_in[:, C_KR:C_KR + 64], 16, 64)

        def ev_kr(ti, i, pt, bk):
            nt = tiles[ti][2]
            evac_copy("act", krw[0:nt, i, 0:64], pt, bk, [("krw", i)])
            rope_tm(krr[0:nt, i, :], krw[0:nt, i, 0:64], ropc[0:nt, i, :], nt, [("krw", i), ("ropc", i)], [("krr", i)])
        proj_tm(wv, wkey, 16, 64, hT, hk, ptiles, ev_kr, after_tile=lambda ti: pump(1))
        wv, wkey = load_slab(w_in[:, C_KV:C_KV + 512], 16, 512)

        def ev_ckv(ti, i, pt, bk):
            nt = tiles[ti][2]
            evac_copy("act" if ti % 2 else "dve", ckvr[0:nt, i, :], pt, bk, [("ckvr", i)])
        proj_tm(wv, wkey, 16, 512, hT, hk, ptiles, ev_ckv, after_tile=lambda ti: pump(2))
        for ti, (i, c0, nt) in enumerate(tiles):
            c = statcol()
            ss = stat[0:nt, c:c + 1]
            S.op("act", lambda e, i=i, nt=nt, ss=ss: e.activation(ckvb[0][0:nt, :], ckvr[0:nt, i, :], AF.Square, accum_out=ss),
                 r=[("ckvr", i)], w=[("stat", c), ("ckvb", 0)])
            act_rstd(ss, ss, 512.0, [("stat", c)], [("stat", c)])
            S.op("dve", lambda e, i=i, nt=nt, ss=ss: e.scalar_tensor_tensor(ckvr[0:nt, i, :], ckvr[0:nt, i, :], ss, gkv_bc[0:nt, :], ALU.mult, ALU.mult),
                 r=[("ckvr", i), ("stat", c), "gkv_bc"], w=[("ckvr", i)])
        if own is not None:
            nt = [t for t in tiles if t[0] == own][0][2]
            dma("act", outs[0], ckvr[0:nt, own, :], [("ckvr", own)], [], ("ckvr_o", own))
            dma("act", outs[1], krr[0:nt, own, :], [("krr", own)], [], ("krr_o", own))
        pump(len(stages))
        if "gla" in KSKIP:
            return
        for h in range(4):
            wvk, wkk = load_slab(w_in[:, C_GK + h * 256:C_GK + (h + 1) * 256], 16, 256)

            def ev_k(ti, i, pt, bk, h=h):
                nt = tiles[ti][2]
                S.op("dve", lambda e: e.tensor_tensor(kend[0:nt, i, :], pt, Ebuf[0:nt, i, h * 256:(h + 1) * 256], ALU.mult),
                     r=[bk, ("Ebuf", i)], w=[("kend", i)])
            proj_tm(wvk, wkk, 16, 256, hT, hk, ptiles, ev_k)
            if own is not None:
                oi, oc0, ont = [t for t in tiles if t[0] == own][0]
                for c in range(2):
                    S.op("act", lambda e, c=c, h=h, ont=ont: e.activation(eqk[:, c * 2, 0:ont], cumT[:, h * 2 + c, 0:ont], AF.Exp, scale=-1.0 / 16), r=[("xt", 0)], w=["eqk"])
                    S.op("act", lambda e, c=c, h=h, ont=ont: e.activation(eqk[:, c * 2 + 1, 0:ont], cumT[:, h * 2 + c, 0:ont], AF.Exp, scale=1.0 / 16), r=[("xt", 0)], w=["eqk"])
                    proj_fm(wvk, wkk, 16, c * 128, 128, hT, [("hT", own)], oc0, ont,
                            lambda pt, bk, c=c, ont=ont: S.op("dve", lambda e: e.tensor_tensor(qkT[:, 2 + c, 0:ont], pt, eqk[:, c * 2 + 1, 0:ont], ALU.mult), r=[bk, "eqk"], w=["qkT"]))
            wv, wkey = load_slab(w_in[:, C_GV + h * 512:C_GV + (h + 1) * 512], 16, 512)

            def ev_v(ti, i, pt, bk):
                nt = tiles[ti][2]
                evac_copy("act" if ti % 2 else "dve", vh[0:nt, i, :], pt, bk, ["vh"])
            proj_tm(wv, wkey, 16, 512, hT, hk, ptiles, ev_v)
            if own is not None:
                wv, wkey = load_slab(w_in[:, C_GQ + h * 256:C_GQ + (h + 1) * 256], 16, 256)
                for c in range(2):
                    proj_fm(wv, wkey, 16, c * 128, 128, hT, [("hT", own)], oc0, ont,
                            lambda pt, bk, c=c, ont=ont: S.op("dve", lambda e: e.scalar_tensor_tensor(qkT[:, c, 0:ont], pt, 1.0 / 16, eqk[:, c * 2, 0:ont], ALU.mult, ALU.mult), r=[bk, "eqk"], w=["qkT"]))
            for ti, (i, c0, nt) in enumerate(tiles):
                if own == i:
                    S.op("act", lambda e, h=h: e.activation(Sbf[:, :, :], Sst[:, h * 2:h * 2 + 2, :], AF.Copy), r=["Sst"], w=["Sbf"])
                    b = rbank()
                    pa = banks[b][0:nt, 0:nt]
                    for c in range(2):
                        S.op("pe", lambda e, pa=pa, c=c, nt=nt: e.matmul(pa, qkT[:, 2 + c, 0:nt], qkT[:, c, 0:nt], start=(c == 0), stop=(c == 1)), r=["qkT"], w=[("bank", b)])
                    S.op("dve", lambda e, pa=pa, nt=nt: e.tensor_tensor(ATb[0:nt, 0:nt], pa, tri_f[0:nt, 0:nt], ALU.mult), r=[("bank", b), "cst"], w=["ATb"])
                    b = rbank()
                    po = banks[b][0:nt, :]
                    S.op("pe", lambda e, po=po, nt=nt, i=i: e.matmul(po, ATb[0:nt, 0:nt], vh[0:nt, i, :], start=True, stop=False), r=["ATb", "vh"], w=[("bank", b)])
                    for c in range(2):
                        S.op("pe", lambda e, po=po, c=c, nt=nt: e.matmul(po, qkT[:, c, 0:nt], Sbf[:, c, :], start=False, stop=(c == 1)), r=["qkT", "Sbf"], w=[("bank", b)])
                    cc = statcol()
                    ss = stat[0:nt, cc:cc + 1]
                    S.op("act", lambda e, po=po, nt=nt, ss=ss: e.activation(onb[0:nt, :], po, AF.Square, accum_out=ss), r=[("bank", b)], w=[("stat", cc), "onb"])
                    act_rstd(ss, ss, 512.0, [("stat", cc)], [("stat", cc)])
                    S.op("dve", lambda e, po=po, nt=nt, ss=ss: e.scalar_tensor_tensor(onb[0:nt, :], po, ss, ggla_bc[0:nt, :], ALU.mult, ALU.mult),
                         r=[("bank", b), ("stat", cc), "ggla_bc"], w=["onb"])
                    b2 = rbank()
                    pv = banks[b2][:, :].bitcast(BF16)
                    for q4 in range(4):
                        S.op("pe", lambda e, pv=pv, q4=q4, nt=nt: e.transpose(pv[:, q4 * 128:q4 * 128 + nt], onb[0:nt, q4 * 128:(q4 + 1) * 128], ident_b[0:nt, 0:nt]),
                             r=["onb", "ident_b"], w=[("bank", b2)])
                    S.op("act", lambda e, pv=pv, h=h, nt=nt: e.activation(ogT[:, h * 4:(h + 1) * 4, 0:nt], pv[:, 0:512].rearrange("p (q c) -> p q c", q=4)[:, :, 0:nt], AF.Copy),
                         r=[("bank", b2)], w=["ogT"])
                for c in range(2):
                    b = rbank()
                    pt = banks[b][:, :]
                    S.op("pe", lambda e, pt=pt, nt=nt, i=i, c=c: e.matmul(pt, kend[0:nt, i, c * 128:(c + 1) * 128], vh[0:nt, i, :], start=True, stop=True),
                         r=[("kend", i), "vh"], w=[("bank", b)])
                    S.op("dve", lambda e, pt=pt, i=i, hc=h * 2 + c: e.scalar_tensor_tensor(Sst[:, hc, :], Sst[:, hc, :], Dt[:, i, hc:hc + 1], pt, ALU.mult, ALU.add),
                         r=[("bank", b), ("Dt", i), "Sst"], w=["Sst"])
        if own is not None:
            dma("act", ogdst, ogT[:, :, 0:ont], ["ogT"], [], "ogT_o")


    dstp = dict(knT=knT_scr, krT=krT_scr, v=v_scr, invk=invk_scr)
    full8 = [(i, i * 128, 128) for i in range(8)]
    def nt_closure(r, i):
        t = 8 * r + i
        return lambda: norm_transpose(xs[t * 128:(t + 1) * 128, :], 128, gmixT, i * 128, ("hT", i))
    for r in range(KR if stage >= 1 else 0):
        if r == 0:
            for i in range(8):
                nt_closure(0, i)()
        if "nofront" not in KSKIP:
            front_round(full8, lambda i, nt, r=r: ropet[(8 * r + i) * 128:(8 * r + i) * 128 + nt, :], 7,
                    (ckv_p[r * 128:(r + 1) * 128, :], kr_p[r * 128:(r + 1) * 128, :]), ogla_scr[r])
        if "nokv" not in KSKIP.split(","):
            kv_finish([(i, 128) for i in range(8)], dstp, 8 * r,
                      fill=[nt_closure(r + 1, i) for i in range(8)] if r + 1 < KR else None)
    dma("act", st_p.rearrange("h (c p) v -> p (h c) v", p=128), Sst[:, :, :], ["Sst"], [], "Sst_o")
    for s_ in range(2 if stage >= 2 else 0):
        dsts = dict(knT=knT_s[s_], krT=krT_s[s_], v=v_s[s_], invk=invk_s[s_])
        for rr in range(2):
            for i in range(8):
                t = 8 * rr + i
                dma("sp", ckvr[:, i, :], cckv[s_, t * 128:(t + 1) * 128, :], [], [("ckvr", i)], ("ckvr_i", i % 2))
                dma("sp", krr[:, i, :], ckr[s_, t * 128:(t + 1) * 128, :], [], [("krr", i)], ("krr_i", i % 2))
            kv_finish([(i, 128) for i in range(8)], dsts, 8 * rr)
        norm_transpose(x_s[s_ * 16:(s_ + 1) * 16, :], 16, gmixT, 0, ("hT", 0))
        dma("sp", Sst[:, :, :], st_in[s_].rearrange("h (c p) v -> p (h c) v", p=128), [], ["Sst"], "Sst_i")
        front_round([(0, 0, 16)], lambda i, nt, s_=s_: ropes[s_ * 16:s_ * 16 + nt, :], 0,
                    (ckv_s[s_ * 16:(s_ + 1) * 16, :], kr_s[s_ * 16:(s_ + 1) * 16, :]), ogla_s[s_])
        kv_finish([(0, 16)], dsts, 16)
        dma("act", st_s[s_].rearrange("h (c p) v -> p (h c) v", p=128), Sst[:, :, :], ["Sst"], [], "Sst_o")


    if stage >= 3:
        S.barrier(NOPS)
        es1.close()
        es2 = new_scope()
        C.bank_rr = 0
        SCALE = 192.0 ** -0.5
        oT = sb("oT", [128, 16, 1056], BF16)
        QrT = sb("QrT", [128, 8, 1056], BF16)
        gqk = sb("gqk", [128, 192], F32)
        gtmp = sb("gtmp", [128, 192], F32)
        gq_bc = sb("gq_bc", [128, 512], F32)
        rope2 = sb("rope2", [128, 9, 64], F32)
        rtmp = sb("rtmp", [128, 4, 32], F32)
        tiles2 = [(ti, ti * 128, 128) for ti in range(8)] + [(8, 1024, 32)]
        dma("sp", gq_bc[:, :], g_q.partition_broadcast(128), [], ["gq_bc"], "c0")
        dma("sp", gqk[:, 0:128], gq_n.partition_broadcast(128), [], ["gqk"], "c0")
        dma("sp", gqk[:, 128:160], gq_r.partition_broadcast(128), [], ["gqk"], "c0")
        dma("sp", gtmp[:, 0:128], gk_n.partition_broadcast(128), [], ["gtmp"], "c0")
        dma("sp", gtmp[:, 128:160], gk_r.partition_broadcast(128), [], ["gtmp"], "c0")
        S.op("dve", lambda e: e.scalar_tensor_tensor(gqk[:, 0:160], gqk[:, 0:160], SCALE, gtmp[:, 0:160], ALU.mult, ALU.mult), r=["gqk", "gtmp"], w=["gqk"])
        S.op("dve", lambda e: e.tensor_copy(gqk[:, 160:192], gqk[:, 128:160]), r=["gqk"], w=["gqk"])
        for k in range(8):
            t = 8 * k + 7
            dma("sp", rope2[:, k, :], ropet[t * 128:(t + 1) * 128, :], [], [("rope2", k)], "c0")
        dma("sp", rope2[0:32, 8, :], ropes, [], [("rope2", 8)], "c0")

        def own_norm_transpose():
            for k in range(8):
                t = 8 * k + 7
                norm_transpose(xs[t * 128:(t + 1) * 128, :], 128, gmixT, k * 128, ("hT", k))
            norm_transpose(x_s, 32, gmixT, 1024, ("hT", 8))

        def rope_ip(x, cs, ntok, rk, wk):
            x1, x2 = x[:, 0:32], x[:, 32:64]
            co, si = cs[:, 0:32], cs[:, 32:64]
            t = rtmp
            S.op("dve", lambda e: e.tensor_tensor(t[0:ntok, 0, :], x1, co, ALU.mult), r=rk, w=["rtmp"])
            S.op("dve", lambda e: e.tensor_tensor(t[0:ntok, 1, :], x2, si, ALU.mult), r=rk, w=["rtmp"])
            S.op("dve", lambda e: e.tensor_tensor(t[0:ntok, 2, :], x2, co, ALU.mult), r=rk, w=["rtmp"])
            S.op("dve", lambda e: e.tensor_tensor(t[0:ntok, 3, :], x1, si, ALU.mult), r=rk, w=["rtmp"])
            S.op("dve", lambda e: e.tensor_tensor(x1, t[0:ntok, 0, :], t[0:ntok, 1, :], ALU.subtract), r=["rtmp"], w=wk)
            S.op("dve", lambda e: e.tensor_tensor(x2, t[0:ntok, 2, :], t[0:ntok, 3, :], ALU.add), r=["rtmp"], w=wk)

        own_norm_transpose()
        hk2 = lambda ti: ("hT", ti)
        ptiles2 = [(c0, nt, ti) for (ti, c0, nt) in tiles2]
        esq = new_scope()
        alloc_wbufs()
        qlr = sb("qlr", [128, 9, 512], BF16)
        qlb = sb("qlb", [128, 512], BF16)
        qlT = sb("qlT", [128, 4, 1056], BF16)
        qh2 = sb("qh2", [128, 2, 192], F32)
        qnb2 = sb("qnb2", [128, 2, 128], BF16)
        qrb = sb("qrb", [128, 128], BF16)
        for half in range(2):
            wv, wkey = load_slab(w_in[:, C_QLAT + half * 256:C_QLAT + (half + 1) * 256], 16, 256)

            def ev_ql(ti, tag, pt, bk, half=half):
                nt = tiles2[ti][2]
                evac_copy("act" if ti % 2 else "dve", qlr[0:nt, ti, half * 256:(half + 1) * 256], pt, bk, [("qlr", ti)])
            proj_tm(wv, wkey, 16, 256, hT, hk2, ptiles2, ev_ql)
        for (ti, c0, nt) in tiles2:
            c = statcol()
            ss = stat[0:nt, c:c + 1]
            S.op("act", lambda e, ti=ti, nt=nt, ss=ss: e.activation(qlb[0:nt, :], qlr[0:nt, ti, :], AF.Square, accum_out=ss), r=[("qlr", ti)], w=[("stat", c), "qlb"])
            act_rstd(ss, ss, 512.0, [("stat", c)], [("stat", c)])
            S.op("dve", lambda e, ti=ti, nt=nt, ss=ss: e.scalar_tensor_tensor(qlb[0:nt, :], qlr[0:nt, ti, :], ss, gq_bc[0:nt, :], ALU.mult, ALU.mult),
                 r=[("qlr", ti), ("stat", c), "gq_bc"], w=["qlb"])
            b = rbank()
            pv = banks[b][:, :].bitcast(BF16)
            for kc in range(4):
                S.op("pe", lambda e, pv=pv, kc=kc, nt=nt: e.transpose(pv[:, kc * 128:kc * 128 + nt], qlb[0:nt, kc * 128:(kc + 1) * 128], ident_b[0:nt, 0:nt]),
                     r=["qlb", "ident_b"], w=[("bank", b)])
            S.op("act", lambda e, pv=pv, c0=c0, nt=nt: e.activation(qlT[:, :, c0:c0 + nt], pv[:, 0:512].rearrange("p (k c) -> p k c", k=4)[:, :, 0:nt], AF.Copy),
                 r=[("bank", b)], w=[("qlT", ti)])
        for sl in range(4):
            wv, wkey = load_slab(w_uq[:, sl * 768:(sl + 1) * 768], 4, 768)
            for (ti, c0, nt) in tiles2:
                for g in range(2):
                    b = rbank()
                    pt = banks[b][0:nt, 0:384]
                    for kc in range(4):
                        S.op("pe", lambda e, pt=pt, kc=kc, c0=c0, nt=nt, g=g, wv=wv: e.matmul(pt, qlT[:, kc, c0:c0 + nt], wv[:, kc, g * 384:(g + 1) * 384], start=(kc == 0), stop=(kc == 3)),
                             r=[wkey, ("qlT", ti)], w=[("bank", b)])
                    S.op("act", lambda e, pt=pt, nt=nt: e.activation(qh2[0:nt, :, :], pt.rearrange("p (a c) -> p a c", a=2), AF.Copy), r=[("bank", b)], w=["qh2"])
                    for hh in range(2):
                        rope_ip(qh2[0:nt, hh, 128:192], rope2[0:nt, ti, :], nt, ["qh2", ("rope2", ti)], ["qh2"])
                        c = statcol()
                        ss = stat[0:nt, c:c + 1]
                        S.op("act", lambda e, nt=nt, hh=hh, ss=ss: e.activation(qlb[0:nt, 0:192], qh2[0:nt, hh, :], AF.Square, accum_out=ss), r=["qh2"], w=[("stat", c), "qlb"])
                        act_rstd(ss, ss, 192.0, [("stat", c)], [("stat", c)])
                        S.op("dve", lambda e, nt=nt, hh=hh, ss=ss: e.scalar_tensor_tensor(qnb2[0:nt, hh, :], qh2[0:nt, hh, 0:128], ss, gqk[0:nt, 0:128], ALU.mult, ALU.mult),
                             r=["qh2", ("stat", c), "gqk"], w=["qhb2"])
                        S.op("dve", lambda e, nt=nt, hh=hh, ss=ss: e.scalar_tensor_tensor(qrb[0:nt, hh * 64:(hh + 1) * 64], qh2[0:nt, hh, 128:192], ss, gqk[0:nt, 128:192], ALU.mult, ALU.mult),
                             r=["qh2", ("stat", c), "gqk"], w=["qhb2"])
                        h = sl * 4 + g * 2 + hh
                        bA = rbank()
                        pvA = banks[bA][:, :].bitcast(BF16)
                        S.op("pe", lambda e, pvA=pvA, nt=nt, hh=hh: e.transpose(pvA[:, 0:nt], qnb2[0:nt, hh, :], ident_b[0:nt, 0:nt]), r=["qhb2", "ident_b"], w=[("bank", bA)])
                        S.op("act", lambda e, pvA=pvA, nt=nt, h=h, c0=c0: e.activation(hT[:, h, c0:c0 + nt], pvA[:, 0:nt], AF.Copy), r=[("bank", bA)], w=[("hT", ti)])
                    bB = rbank()
                    pvB = banks[bB][:, :].bitcast(BF16)
                    S.op("pe", lambda e, pvB=pvB, nt=nt: e.transpose(pvB[:, 0:nt], qrb[0:nt, :], ident_b[0:nt, 0:nt]), r=["qhb2", "ident_b"], w=[("bank", bB)])
                    S.op("dve", lambda e, pvB=pvB, nt=nt, pr=sl * 2 + g, c0=c0: e.tensor_copy(QrT[:, pr, c0:c0 + nt], pvB[:, 0:nt]), r=[("bank", bB)], w=[("QrT", ti)])
        S.barrier(NOPS)
        esq.close()
        esa = new_scope()
        C.RB = 4
        C.bank_rr = 0
        krT_all = sb("krT_all", [128, T], BF16)
        invk_all = sb("invk_all", [128, NT * 16], F32)
        KTb = [sb("KT%d" % i, [128, 4096], BF16) for i in range(2)]
        Vb = [sb("V%d" % i, [128, 32, 128], BF16) for i in range(2)]
        Pb = [sb("Pb%d" % i, [128, 512], BF16) for i in range(3)]
        rec = sb("rec", [128, 512], F32)
        KTs = sb("KTs", [128, 17 * 128], BF16)
        Vs = sb("Vs", [128, 17, 128], BF16)
        krTs = sb("krTs", [128, 17 * 128], BF16)
        invks = sb("invks", [128, 17 * 16], F32)
        dma("sp", krT_all[0:64, :], krT_scr, [], ["krT_all"], "krT_all0")
        dma("sp", krT_all[64:128, :], krT_scr, [], ["krT_all"], "krT_all1")
        dma("sp", invk_all[:, :], invk_scr.rearrange("p j h -> p (j h)"), [], ["invk_all"], "invk_all")
        OTp, Lp = banks[4], banks[5]
        C.pj = 0

        SKEW = 2

        def attend(h, keytiles, qcol0, nq, kt_of, v_of, krt, ik_of, kb_of, diag_of, okeys):
            rp = (h % 2) * 64
            nk = len(keytiles)
            pjs = {}

            def scores(idx):
                j, c0, n = keytiles[idx]
                b = rbank()
                pt = banks[b][:, 0:n]
                ktap, ktkey = kt_of(j)
                S.op("pe", lambda e: e.matmul(pt, ktap, hT[:, h, qcol0 + c0:qcol0 + c0 + n], start=True, stop=False),
                     r=[ktkey] + okeys, w=[("bank", b)])
                S.op("pe", lambda e: e.matmul(pt, krt[rp:rp + 64, j * 128:(j + 1) * 128], QrT[rp:rp + 64, h // 2, qcol0 + c0:qcol0 + c0 + n], start=False, stop=True),
                     r=["krT_all", "krTs"], w=[("bank", b)])
                pj = C.pj = (C.pj + 1) % 3
                pjs[idx] = pj
                S.op("act", lambda e: e.activation(Pb[pj][:, 0:n], pt, AF.Exp, bias=kb_of(j), scale=ik_of(j)),
                     r=[("bank", b), "invk_all", "invks", "kb_sb"], w=[("Pb", pj)])
                if diag_of(j):
                    S.op("dve", lambda e: e.memset(Pb[pj][64:128, 0:64], 0.0), w=[("Pb", pj)])

            def pv(idx):
                j, c0, n = keytiles[idx]
                pj = pjs[idx]
                vap, vkey = v_of(j)
                S.op("pe", lambda e: e.matmul(OTp[:, c0:c0 + n], vap, Pb[pj][:, 0:n], start=(idx == 0), stop=(idx == nk - 1)),
                     r=[vkey, ("Pb", pj)], w=["OT"])
                S.op("pe", lambda e: e.matmul(Lp[:, c0:c0 + n], ones_b[:, :], Pb[pj][:, 0:n], start=(idx == 0), stop=(idx == nk - 1)),
                     r=["ones_b", ("Pb", pj)], w=["L"])

            for idx in range(nk + SKEW):
                if idx < nk:
                    scores(idx)
                if idx >= SKEW:
                    pv(idx - SKEW)
            S.op("dve", lambda e: e.reciprocal(rec[:, 0:nq], Lp[:, 0:nq]), r=["L"], w=["rec"])
            S.op("dve", lambda e: e.tensor_tensor(oT[:, h, qcol0:qcol0 + nq], OTp[:, 0:nq], rec[:, 0:nq], ALU.mult), r=["OT", "rec"], w=[("oT", h)])

        for h in range(16 if "noattn" not in KSKIP else 0):
            dma("sp", KTb[0][:, :], knT_scr[h, :, 0:4096], [], [("KT", 0)], ("KT", 0))
            dma("sp", Vb[0][:, :, :], v_scr[h, :, 0:32, :], [], [("V", 0)], ("V", 0))
            dma("sp", KTb[1][:, :], knT_scr[h, :, 4096:8192], [], [("KT", 1)], ("KT", 1))
            dma("sp", Vb[1][:, :, :], v_scr[h, :, 32:64, :], [], [("V", 1)], ("V", 1))
            for qhalf in range(2):
                kts = []
                for rr in range(4 * qhalf + 4):
                    for i in range(8):
                        kmin = max(rr, 4 * qhalf)
                        c0 = (kmin - 4 * qhalf) * 128
                        kts.append((8 * rr + i, c0, 512 - c0))
                attend(h, kts, qhalf * 512, 512,
                       lambda j: (KTb[j // 32][:, (j % 32) * 128:(j % 32 + 1) * 128], ("KT", j // 32)),
                       lambda j: (Vb[j // 32][:, j % 32, :], ("V", j // 32)),
                       krT_all,
                       lambda j, h=h: invk_all[:, j * 16 + h:j * 16 + h + 1],
                       lambda j: kb_sb[:, j:j + 1],
                       lambda j, qhalf=qhalf: (j % 8 == 7) and (j // 8 >= 4 * qhalf),
                       [("hT", k) for k in range(8)] + [("QrT", k) for k in range(8)])
        for s_ in range(2 if "noattn" not in KSKIP else 0):
            dma("sp", krTs[0:64, :], krT_s[s_], [], ["krTs"], "krTs0")
            dma("sp", krTs[64:128, :], krT_s[s_], [], ["krTs"], "krTs1")
            dma("sp", invks[:, :], invk_s[s_].rearrange("p j h -> p (j h)"), [], ["invks"], "invks")
            for h in range(16):
                dma("sp", KTs[:, :], knT_s[s_, h], [], ["KTs"], "KTs")
                dma("sp", Vs[:, :, :], v_s[s_, h], [], ["Vs"], "Vs")
                attend(h, [(j, 0, 16) for j in range(17)], 1024 + 16 * s_, 16,
                       lambda j: (KTs[:, j * 128:(j + 1) * 128], "KTs"),
                       lambda j: (Vs[:, j, :], "Vs"),
                       krTs,
                       lambda j, h=h: invks[:, j * 16 + h:j * 16 + h + 1],
                       lambda j: kb_sb[:, NT:NT + 1] if j == 16 else kb_sb[:, 7:8],
                       lambda j: False,
                       [("hT", 8), ("QrT", 8)])


        S.barrier(NOPS)
        esa.close()
        esb = new_scope()
        C.RB = 7
        C.bank_rr = 0
        alloc_wbufs()
        ogc = [sb("ogc%d" % i, [128, 544], BF16) for i in range(2)]
        tA = sb("tA", [128, 512], F32)
        tB = sb("tB", [128, 512], BF16)
        tC = sb("tC", [128, 512], BF16)
        x1 = sb("x1", [128, 5, D], F32)
        actg = sb("actg", [128, 8, 544], BF16)
        own_norm_transpose()
        for half in range(2 if "noback" not in KSKIP else 0):
            if half == 0:
                htiles = [(k, k * 128, 128, k) for k in range(4)]
                pieces = [(0, 512)]
                hc0, hn = 0, 512
            else:
                htiles = [(k, k * 128, 128, k - 4) for k in range(4, 8)] + [(8, 1024, 32, 4)]
                pieces = [(512, 512), (1024, 32)]
                hc0, hn = 512, 544
            hkeys = [("hT", t[0]) for t in htiles]
            for c in range(16):
                og = ogc[c % 2]
                for (k, c0a, nt, slot) in htiles:
                    if k < 8:
                        dma("sp", og[:, c0a - hc0:c0a - hc0 + 128], ogla_scr[k, :, c, :], [], [("ogc", c % 2)], ("ogc", c % 2, k % 2))
                    else:
                        for s_ in range(2):
                            dma("sp", og[:, 512 + s_ * 16:512 + (s_ + 1) * 16], ogla_s[s_, :, c, :], [], [("ogc", c % 2)], ("ogc", c % 2, s_))
                wog, kog = load_slab(w_in[:, C_OG + c * 128:C_OG + (c + 1) * 128], 16, 128)
                wgg, kgg = load_slab(w_in[:, C_GG + c * 128:C_GG + (c + 1) * 128], 16, 128)
                wgm, kgm = load_slab(w_in[:, C_GM + c * 128:C_GM + (c + 1) * 128], 16, 128)
                for (n0, n) in pieces:
                    proj_fm(wog, kog, 16, 0, 128, hT, hkeys, n0, n,
                            lambda pt, bk, n=n: S.op("act", lambda e: e.activation(tA[:, 0:n], pt, AF.Silu), r=[bk], w=["tA"]))
                    proj_fm(wgg, kgg, 16, 0, 128, hT, hkeys, n0, n,
                            lambda pt, bk, n=n: S.op("act", lambda e: e.activation(tB[:, 0:n], pt, AF.Sigmoid), r=[bk], w=["tB"]))
                    proj_fm(wgm, kgm, 16, 0, 128, hT, hkeys, n0, n,
                            lambda pt, bk, n=n: S.op("act", lambda e: e.activation(tC[:, 0:n], pt, AF.Sigmoid), r=[bk], w=["tC"]))
                    S.op("dve", lambda e, n=n: e.tensor_tensor(tA[:, 0:n], tA[:, 0:n], tB[:, 0:n], ALU.mult), r=["tA", "tB"], w=["tA"])
                    S.op("dve", lambda e, n=n, r0=n0 - hc0, og=og: e.tensor_tensor(tA[:, 0:n], tA[:, 0:n], og[:, r0:r0 + n], ALU.mult), r=["tA", ("ogc", c % 2)], w=["tA"])
                    S.op("dve", lambda e, n=n, n0=n0, c=c: e.tensor_tensor(tC[:, 0:n], tC[:, 0:n], oT[:, c, n0:n0 + n], ALU.mult), r=["tC", ("oT", c)], w=["tC"])
                    S.op("dve", lambda e, n=n, n0=n0, c=c: e.tensor_tensor(oT[:, c, n0:n0 + n], tA[:, 0:n], tC[:, 0:n], ALU.add), r=["tA", "tC"], w=[("oT", c)])
            for (k, c0a, nt, slot) in htiles:
                src = xs[(8 * k + 7) * 128:(8 * k + 8) * 128, :] if k < 8 else x_s
                dma("sp", x1[0:nt, slot, :], src, [], [("x1", slot)], ("x1", slot % 2))
            wtiles = [(c0a, nt, slot) for (k, c0a, nt, slot) in htiles]
            okeys = [("oT", c) for c in range(16)]
            for sl in range(8):
                wv, wkey = load_slab(w_o[:, sl * 256:(sl + 1) * 256], 16, 256)

                def ev_o(ti, slot, pt, bk, sl=sl):
                    nt = wtiles[ti][1]
                    S.op("dve", lambda e: e.tensor_tensor(x1[0:nt, slot, sl * 256:(sl + 1) * 256], x1[0:nt, slot, sl * 256:(sl + 1) * 256], pt, ALU.add),
                         r=[bk, ("x1", slot)], w=[("x1", slot)])
                for ti, (c0a, nt, slot) in enumerate(wtiles):
                    b = rbank()
                    pt = banks[b][0:nt, 0:256]
                    for kc in range(16):
                        S.op("pe", lambda e, pt=pt, kc=kc, c0a=c0a, nt=nt, wv=wv: e.matmul(pt, oT[:, kc, c0a:c0a + nt], wv[:, kc, :], start=(kc == 0), stop=(kc == 15)),
                             r=[wkey] + okeys, w=[("bank", b)])
                    ev_o(ti, slot, pt, ("bank", b))
            for (k, c0a, nt, slot) in htiles:
                c = statcol()
                ss = stat[0:nt, c:c + 1]
                S.op("act", lambda e, nt=nt, slot=slot, ss=ss: e.activation(hb[0][0:nt, :], x1[0:nt, slot, :], AF.Square, accum_out=ss), r=[("x1", slot)], w=[("stat", c), ("hb", 0)])
                act_rstd(ss, ss, float(D), [("stat", c)], [("stat", c)])
                S.op("dve", lambda e, nt=nt, slot=slot, ss=ss: e.tensor_scalar(hb[0][0:nt, :], x1[0:nt, slot, :], ss, None, ALU.mult), r=[("x1", slot), ("stat", c)], w=[("hb", 0)])
                for g4 in range(4):
                    b = rbank()
                    pv = banks[b][:, :].bitcast(BF16)
                    for j in range(4):
                        kc = g4 * 4 + j
                        S.op("pe", lambda e, pv=pv, j=j, kc=kc, nt=nt: e.transpose(pv[:, j * 128:j * 128 + nt], hb[0][0:nt, kc * 128:(kc + 1) * 128], ident_b[0:nt, 0:nt]),
                             r=[("hb", 0), "ident_b"], w=[("bank", b)])
                    for j in range(4):
                        kc = g4 * 4 + j
                        S.op("act", lambda e, pv=pv, j=j, kc=kc, nt=nt, c0a=c0a: e.activation(hT[:, kc, c0a:c0a + nt], pv[:, j * 128:j * 128 + nt], AF.Copy, scale=gffnT[:, kc:kc + 1]),
                             r=[("bank", b), "gffnT"], w=[("hT", k)])
            rtiles = [(c0a - hc0, nt, slot) for (k, c0a, nt, slot) in htiles]
            for g in range(8):
                for sl in range(4):
                    wv, wkey = load_slab(w_up[:, g * 1024 + sl * 256:g * 1024 + (sl + 1) * 256], 16, 256)
                    for cc in range(2):
                        for (n0, n) in pieces:
                            proj_fm(wv, wkey, 16, cc * 128, 128, hT, hkeys, n0, n,
                                    lambda pt, bk, r0=n0 - hc0, n=n, ch=sl * 2 + cc: (
                                        S.op("act", lambda e: e.activation(tA[:, 0:n], pt, AF.Relu), r=[bk], w=["tA"]),
                                        S.op("dve", lambda e: e.tensor_tensor(actg[:, ch, r0:r0 + n], tA[:, 0:n], tA[:, 0:n], ALU.mult), r=["tA"], w=[("actg", ch)])))
                akeys = [("actg", ch) for ch in range(8)]
                for sl in range(4):
                    wv, wkey = load_slab(w_dn[g * 1024:(g + 1) * 1024, sl * 512:(sl + 1) * 512], 8, 512)
                    for ti, (c0r, nt, slot) in enumerate(rtiles):
                        b = rbank()
                        pt = banks[b][0:nt, 0:512]
                        for kc in range(8):
                            S.op("pe", lambda e, pt=pt, kc=kc, c0r=c0r, nt=nt, wv=wv: e.matmul(pt, actg[:, kc, c0r:c0r + nt], wv[:, kc, :], start=(kc == 0), stop=(kc == 7)),
                                 r=[wkey] + akeys, w=[("bank", b)])
                        S.op("dve", lambda e, pt=pt, nt=nt, slot=slot, sl=sl: e.tensor_tensor(x1[0:nt, slot, sl * 512:(sl + 1) * 512], x1[0:nt, slot, sl * 512:(sl + 1) * 512], pt, ALU.add),
                             r=[("bank", b), ("x1", slot)], w=[("x1", slot)])
            for (k, c0a, nt, slot) in htiles:
                dst = y_p[k * 128:(k + 1) * 128, :] if k < 8 else y_s
                dma("act", dst, x1[0:nt, slot, :], [("x1", slot)], [], ("x1_o", slot % 2))

    if "bar" in KSKIP:
        S.barrier(NOPS)
    if "nodbg" in KSKIP:
        pass
    elif "dbgA" in KSKIP:
        dma("sp", hb[0][:, 0:128], xs[0:128, 0:64].bitcast(BF16), [], [("hb", 0)], "dbg0")
        dma("sp", xt[0][:, 0:16], xs[0:128, 0:16], [], [("xt", 0)], "dbg1")
    elif "dbgsmall" in KSKIP:
        dma("sp", hb[0][0:64, 128:256], krT_scr[:, 0:128], ["scr_krT"], [("hb", 0)], "dbg0")
        dma("sp", xt[0][:, 0:16], invk_scr[:, 0, :], ["scr_invk"], [("xt", 0)], "dbg1")
    elif stage < 3:
        dma("sp", hb[0][:, 0:128], knT_scr[0, :, 0:128], ["scr_knT"], [("hb", 0)], "dbg0")
        dma("sp", hb[0][0:64, 128:256], krT_scr[:, 0:128], ["scr_krT"], [("hb", 0)], "dbg0")
        dma("sp", hb[0][:, 256:384], v_scr[0, :, 0, :], ["scr_v"], [("hb", 0)], "dbg0")
        dma("sp", xt[0][:, 0:16], invk_scr[:, 0, :], ["scr_invk"], [("xt", 0)], "dbg1")
        dma("sp", hb[0][:, 384:512], ogla_scr[0, :, 0, :], [], [("hb", 0)], "dbg0")
        if stage >= 2:
            dma("sp", hb[0][:, 0:128], knT_s[0, 0, :, 0:128], ["scr_knT"], [("hb", 0)], "dbg0")
            dma("sp", hb[0][0:64, 128:256], krT_s[0, :, 0:128], ["scr_krT"], [("hb", 0)], "dbg0")
            dma("sp", hb[0][:, 256:384], v_s[0, 0, :, 0, :], ["scr_v"], [("hb", 0)], "dbg0")
            dma("sp", xt[0][:, 0:16], invk_s[0, :, 0, :], ["scr_invk"], [("xt", 0)], "dbg1")
            dma("sp", hb[0][:, 384:400], ogla_s[0, :, 0, :], [], [("hb", 0)], "dbg0")

    C.S, C.nc, C.es = S, nc, es
    C.__dict__.update(dict(locals()))
    return C


def finish_program(C):
    from contextlib import ExitStack
    nc, S = C.nc, C.S
    for sc in reversed(getattr(C, "scopes", [])):
        sc.close()
    C.cur = C.es
    S.assign()
    keys = set()
    for e in S.ENGS:
        for I in S.ins[e]:
            if I.signal:
                keys.add(I.semkey)
    sems = {k: C.es.enter_context(nc.semaphore("s_%d" % i)) for i, k in enumerate(sorted(keys, key=str))}
    with nc.Block() as block:
        @block.tensor
        def _(e):
            S.emit_engine("pe", e, sems)

        @block.vector
        def _(e):
            S.emit_engine("dve", e, sems)

        @block.scalar
        def _(e):
            S.emit_engine("act", e, sems)

        @block.gpsimd
        def _(e):
            S.emit_engine("pool", e, sems)

        @block.sync
        def _(e):
            S.emit_engine("sp", e, sems)
    C.es.close()
    return nc, len(sems)


def _rope_table(pos):
    half = 32
    freqs = np.power(np.float32(10000.0), -np.arange(half, dtype=np.float32) / np.float32(half)).astype(np.float32)
    ang = pos.astype(np.float32)[:, None] * freqs[None, :]
    return np.concatenate([np.cos(ang), np.sin(ang)], axis=1).astype(np.float32)


def make_in_maps(inp):
    f = lambda k: np.ascontiguousarray(inp[k], dtype=np.float32)
    xp = f("x_prompt")[0]
    xsmp = f("x_sample")
    idn = np.eye(128, dtype=np.float32)
    sidx, tidx = np.meshgrid(np.arange(128), np.arange(128), indexing="ij")
    tri = (sidx <= tidx).astype(np.float32)
    aft = (sidx > tidx).astype(np.float32)
    consts = np.concatenate([idn, tri, aft], axis=1)
    shared = dict(
        consts=consts, w_in=f("w_in")[0], w_uq=f("mla_w_uq")[0].reshape(512, 3072), w_ukv=f("mla_w_ukv")[0].reshape(512, 4096),
        w_a2=f("gla_w_a2")[0], b_a=f("gla_b_a"), w_o=f("w_o")[0], w_up=f("ffn_w_up")[0], w_dn=f("ffn_w_down")[0],
        g_mix=f("norm_mix_g"), g_ffn=f("norm_ffn_g"), g_q=f("mla_q_norm_g"), g_kv=f("mla_kv_norm_g"), g_gla=f("gla_norm_g"),
        gq_n=f("mla_q_gain_nope"), gk_n=f("mla_k_gain_nope"), gq_r=f("mla_q_gain_rope"), gk_r=f("mla_k_gain_rope"))
    maps = []
    for c in range(NCORES):
        xs = np.zeros((NT * 128, D), np.float32)
        pos = np.zeros((NT * 128,), np.float32)
        kb = np.zeros((128, NT + 1), np.float32)
        for j in range(NT):
            t = c - 7 + j
            if t >= 0:
                xs[j * 128:(j + 1) * 128] = xp[t * 128:(t + 1) * 128]
                pos[j * 128:(j + 1) * 128] = t * 128 + np.arange(128)
            else:
                kb[:, j] = NEG
        kb[16:, NT] = NEG
        m = dict(shared)
        m.update(xs=xs, ropet=_rope_table(pos), kbias=kb,
                 x_s=np.ascontiguousarray(xsmp[2 * c:2 * c + 2].reshape(32, D)),
                 ropes=_rope_table(np.tile(2048.0 + np.arange(16, dtype=np.float32), 2)),
                 cckv=np.ascontiguousarray(f("cache_mla_ckv")[0, 2 * c:2 * c + 2]),
                 ckr=np.ascontiguousarray(f("cache_mla_krope")[0, 2 * c:2 * c + 2]),
                 st_in=np.ascontiguousarray(f("state_gla")[0, 2 * c:2 * c + 2]))
        maps.append(m)
    return maps


def assemble(results):
    y_p = np.zeros((1, T, D), np.float32)
    ckv_p = np.zeros((1, 1, T, 512), np.float32)
    kr_p = np.zeros((1, 1, T, 64), np.float32)
    y_s = np.zeros((16, 16, D), np.float32)
    ckv_s = np.zeros((1, 16, 16, 512), np.float32)
    kr_s = np.zeros((1, 16, 16, 64), np.float32)
    st_s = np.zeros((1, 16, 4, 256, 512), np.float32)
    class _Z(dict):
        def __missing__(self, k):
            shp = {"y_p": (1024, D), "ckv_p": (1024, 512), "kr_p": (1024, 64), "st_p": (4, 256, 512), "y_s": (32, D),
                   "ckv_s": (32, 512), "kr_s": (32, 64), "st_s": (2, 4, 256, 512)}[k]
            return np.zeros(shp, np.float32)
    results = [_Z(r) for r in results]
    for c, r in enumerate(results):
        for k in range(8):
            t = c + 8 * k
            y_p[0, t * 128:(t + 1) * 128] = r["y_p"][k * 128:(k + 1) * 128]
            ckv_p[0, 0, t * 128:(t + 1) * 128] = r["ckv_p"][k * 128:(k + 1) * 128]
            kr_p[0, 0, t * 128:(t + 1) * 128] = r["kr_p"][k * 128:(k + 1) * 128]
        y_s[2 * c:2 * c + 2] = r["y_s"].reshape(2, 16, D)
        ckv_s[0, 2 * c:2 * c + 2] = r["ckv_s"].reshape(2, 16, 512)
        kr_s[0, 2 * c:2 * c + 2] = r["kr_s"].reshape(2, 16, 64)
        st_s[0, 2 * c:2 * c + 2] = r["st_s"]
    st_p = np.ascontiguousarray(results[7]["st_p"], dtype=np.float32).reshape(1, 1, 4, 256, 512)
    return (y_p, y_s, ckv_p, kr_p, st_p, ckv_s, kr_s, st_s)


def kernel(**inputs):
    C = build_program()
    nc, _ = finish_program(C)
    res = run_bass_kernel_spmd(nc, make_in_maps(inputs), core_ids=list(range(NCORES)))
    return assemble(res.results)
```

```python
import numpy as np
import concourse.bass as bass
import concourse.mybir as mybir
from concourse.bass_utils import run_bass_kernel_spmd

F32 = mybir.dt.float32
BF16 = mybir.dt.bfloat16
AF = mybir.ActivationFunctionType
ALU = mybir.AluOpType
AX = mybir.AxisListType

NCORES = 8
D = 2048
T = 8192
NT = 64
DIN = 11344
EPS = 1e-6
C_QLAT, C_KV, C_KR, C_GQ, C_GK, C_GV, C_ALR, C_OG, C_GM, C_GG = 0, 512, 1024, 1088, 2112, 3136, 5184, 5200, 7248, 9296
NEG = -30000.0


class Ins:
    __slots__ = ("eng", "fn", "deps", "signal", "val", "semkey", "inc")

    def __init__(self, eng, fn, semkey, inc):
        self.eng, self.fn, self.semkey, self.inc = eng, fn, semkey, inc
        self.deps = []
        self.signal = False
        self.val = None


class Sched:
    ENGS = ("pe", "dve", "act", "pool", "sp")

    def __init__(self):
        self.ins = {e: [] for e in self.ENGS}
        self.lastw = {}
        self.readers = {}
        self.chan_last = {}
        self.all_dma = []

    def op(self, eng, fn, r=(), w=(), chan=None):
        if chan is not None:
            I = Ins(eng, fn, ("ch", chan), 16)
            I.signal = True
        else:
            I = Ins(eng, fn, eng, 1)
        deps = []
        for k in r:
            x = self.lastw.get(k)
            if x is not None:
                deps.append(x)
        for k in w:
            x = self.lastw.get(k)
            if x is not None:
                deps.append(x)
            deps.extend(self.readers.get(k, ()))
        if chan is not None:
            x = self.chan_last.get(chan)
            if x is not None:
                deps.append(x)
            self.chan_last[chan] = I
            self.all_dma.append(I)
        seen = set()
        for d in deps:
            if d is I or id(d) in seen:
                continue
            seen.add(id(d))
            if d.eng == "pe" and eng == "pe" and d.semkey == "pe":
                continue
            d.signal = True
            I.deps.append(d)
        for k in w:
            self.lastw[k] = I
            self.readers[k] = []
        for k in r:
            if k not in w:
                self.readers.setdefault(k, []).append(I)
        self.ins[eng].append(I)
        return I

    def barrier(self, nopfn):
        lasts = []
        for e in self.ENGS:
            if e != "sp" and self.ins[e]:
                for I in reversed(self.ins[e]):
                    if I.semkey == e:
                        lasts.append(I)
                        break
        lasts.extend(self.chan_last.values())
        for e in self.ENGS:
            I = Ins(e, nopfn[e], e, 1)
            for d in lasts:
                if d.eng == e and d.semkey == e and e == "pe":
                    continue
                d.signal = True
                I.deps.append(d)
            self.ins[e].append(I)
        self.lastw = {}
        self.readers = {}

    def assign(self):
        cnt = {}
        for e in self.ENGS:
            for I in self.ins[e]:
                if I.signal:
                    cnt[I.semkey] = cnt.get(I.semkey, 0) + I.inc
                    I.val = cnt[I.semkey]
        self.finals = dict((k, v) for k, v in cnt.items() if isinstance(k, tuple))
        self.cnt = cnt

    def emit_engine(self, e, eng, sems):
        seen = {}
        for I in self.ins[e]:
            need = {}
            for d in I.deps:
                if d.val > need.get(d.semkey, 0):
                    need[d.semkey] = d.val
            for k, v in need.items():
                if seen.get(k, 0) < v:
                    eng.wait_ge(sems[k], v)
                    seen[k] = v
            bi = I.fn(eng)
            if I.signal:
                bi.then_inc(sems[I.semkey], I.inc)
        if e == "sp":
            for k, v in self.finals.items():
                if seen.get(k, 0) < v:
                    eng.wait_ge(sems[k], v)


class Ctx:
    pass


def _bcast_rows(ap_row, nparts):
    return ap_row.partition_broadcast(nparts)


def build_program(stage=3, dbg=False):
    import os
    KR = int(os.environ.get("KR", "8"))
    KSKIP = os.environ.get("KSKIP", "nodbg")
    from contextlib import ExitStack
    nc = bass.Bass("TRN2", target_bir_lowering=False)
    S = Sched()
    es = ExitStack()
    C = Ctx()

    def din(name, shape, dt=F32):
        return nc.dram_tensor(name, list(shape), dt, kind="ExternalInput").ap()

    def dout(name, shape, dt=F32):
        return nc.dram_tensor(name, list(shape), dt, kind="ExternalOutput").ap()

    def dscr(name, shape, dt):
        return nc.dram_tensor(name, list(shape), dt, kind="ExternalOutput").ap()

    C.cur = es
    C.uid = 0
    C.scopes = []

    def new_scope():
        sc = ExitStack()
        C.scopes.append(sc)
        C.cur = sc
        return sc

    def sb(name, shape, dt):
        C.uid += 1
        return C.cur.enter_context(nc.sbuf_tensor("%s_%d" % (name, C.uid), list(shape), dt))

    def ps(name, shape, dt):
        return es.enter_context(nc.psum_tensor(name, list(shape), dt))

    xs = din("xs", [NT * 128, D])
    ropet = din("ropet", [NT * 128, 64])
    kbias = din("kbias", [128, NT + 1])
    consts = din("consts", [128, 3 * 128])
    w_in = din("w_in", [D, DIN])
    w_uq = din("w_uq", [512, 16 * 192])
    w_ukv = din("w_ukv", [512, 16 * 256])
    w_a2 = din("w_a2", [16, 1024])
    b_a = din("b_a", [1, 1024])
    w_o = din("w_o", [D, D])
    w_up = din("w_up", [D, 4 * D])
    w_dn = din("w_dn", [4 * D, D])
    g_mix = din("g_mix", [1, D])
    g_ffn = din("g_ffn", [1, D])
    g_q = din("g_q", [1, 512])
    g_kv = din("g_kv", [1, 512])
    g_gla = din("g_gla", [1, 512])
    gq_n = din("gq_n", [1, 128])
    gk_n = din("gk_n", [1, 128])
    gq_r = din("gq_r", [1, 32])
    gk_r = din("gk_r", [1, 32])
    x_s = din("x_s", [32, D])
    ropes = din("ropes", [32, 64])
    cckv = din("cckv", [2, 2048, 512])
    ckr = din("ckr", [2, 2048, 64])
    st_in = din("st_in", [2, 4, 256, 512])

    y_p = dout("y_p", [1024, D])
    ckv_p = dout("ckv_p", [1024, 512])
    kr_p = dout("kr_p", [1024, 64])
    st_p = dout("st_p", [4, 256, 512])
    y_s = dout("y_s", [32, D])
    ckv_s = dout("ckv_s", [32, 512])
    kr_s = dout("kr_s", [32, 64])
    st_s = dout("st_s", [2, 4, 256, 512])

    knT_scr = dscr("knT_scr", [16, 128, T], BF16)
    krT_scr = dscr("krT_scr", [64, T], BF16)
    v_scr = dscr("v_scr", [16, 128, NT, 128], BF16)
    invk_scr = dscr("invk_scr", [128, NT, 16], F32)
    ogla_scr = dscr("ogla_scr", [8, 128, 16, 128], BF16)

    cst = sb("cst", [128, 384], F32)
    ident_f = cst[:, 0:128]
    tri_f = cst[:, 128:256]
    aft_f = cst[:, 256:384]
    ident_b = sb("ident_b", [128, 128], BF16)
    ones_b = sb("ones_b", [128, 128], BF16)
    ones_f = sb("ones_f", [128, 128], F32)
    gmixT = sb("gmixT", [128, 16], F32)
    gffnT = sb("gffnT", [128, 16], F32)
    gkv_bc = sb("gkv_bc", [128, 512], F32)
    ggla_bc = sb("ggla_bc", [128, 512], F32)
    wa2_f = sb("wa2_f", [17, 1024], F32)
    kb_sb = sb("kb_sb", [128, NT + 1], F32)

    def dma(q, out, in_, r, w, chan, slow=False):
        if any(str(k).startswith("scr_") for k in w):
            if "spq" in KSKIP:
                q = "sp"
            if "kvdma" in KSKIP or any(tok and tok in str(chan) for tok in KSKIP.split(",")):
                return None
        return S.op(q, lambda e, o=out, i=in_: e.dma_start(out=o, in_=i, allow_slow_non_contiguous=slow), r=r, w=w, chan=chan)

    dma("sp", cst[:, :], consts, [], ["cst"], "c0")
    dma("sp", gmixT[:, :], g_mix.rearrange("o (k p) -> p (o k)", p=128), [], ["gmixT"], "c0", slow=True)
    dma("sp", gffnT[:, :], g_ffn.rearrange("o (k p) -> p (o k)", p=128), [], ["gffnT"], "c0", slow=True)
    dma("sp", gkv_bc[:, :], g_kv.partition_broadcast(128), [], ["gkv_bc"], "c0")
    dma("sp", ggla_bc[:, :], g_gla.partition_broadcast(128), [], ["ggla_bc"], "c0")
    dma("sp", wa2_f[0:16, :], w_a2, [], ["wa2_f"], "c0")
    dma("sp", wa2_f[16:17, :], b_a, [], ["wa2_f"], "c0")
    dma("sp", kb_sb[:, :], kbias, [], ["kb_sb"], "c0")
    S.op("dve", lambda e: e.tensor_copy(ident_b[:, :], ident_f), r=["cst"], w=["ident_b"])
    S.op("pool", lambda e: e.memset(ones_b[:, :], 1.0), w=["ones_b"])
    S.op("pool", lambda e: e.memset(ones_f[:, :], 1.0), w=["ones_f"])

    banks = [ps("bank%d" % i, [128, 512], F32) for i in range(8)]
    C.bank_rr = 0
    bar_t = sb("bar_t", [128, 8], F32)
    NOPS = {e_: (lambda e: e.nop()) for e_ in ("pe", "dve", "act", "pool", "sp")}

    def nextbank():
        b = C.bank_rr
        C.bank_rr = (C.bank_rr + 1) % 8
        return b

    def alloc_wbufs(nbuf=5, size=4096):
        C.wbf = [sb("wbf%d" % i, [128, size], BF16) for i in range(nbuf)]
        C.nwb, C.wsize = nbuf, size
        C.wbf_rr = 0
    C.wbf_rr = 0

    def load_slab(src_ap, kcs, ncols):
        wbf = C.wbf
        bi = C.wbf_rr
        C.wbf_rr = (bi + 1) % C.nwb
        n = kcs * ncols
        assert n <= C.wsize
        dv = wbf[bi][:, 0:n].rearrange("p (k c) -> p k c", k=kcs)
        srcv = src_ap.rearrange("(k p) c -> p k c", p=128)
        dma("pool", dv, srcv, [], [("wbf", bi)], ("wbf", bi))
        return dv, ("wbf", bi)

    xt = [sb("xt%d" % i, [128, D], F32) for i in range(1)]
    junk = sb("junk", [128, 64], BF16)
    hb = [sb("hb%d" % i, [128, D], BF16) for i in range(1)]
    hT = sb("hT", [128, 16, 1056], BF16)
    stat = sb("stat", [128, 64], F32)
    C.stat_rr = 0

    def statcol():
        c = C.stat_rr
        C.stat_rr = (c + 1) % 64
        return c

    def norm_transpose(x_rows_ap, ntok, gT, col0, hT_key, xkeep=None):
        i = 0
        dma("sp", xt[i][0:ntok, :], x_rows_ap, [], [("xt", i)], ("xt", i))
        c = statcol()
        ss = stat[0:ntok, c:c + 1]
        S.op("act", lambda e: e.activation(hb[i][0:ntok, :], xt[i][0:ntok, :], AF.Square, accum_out=ss),
             r=[("xt", i)], w=[("stat", c), ("hb", i)])
        S.op("act", lambda e: e.activation(ss, ss, AF.Ln, bias=EPS, scale=1.0 / D), r=[("stat", c)], w=[("stat", c)])
        S.op("act", lambda e: e.activation(ss, ss, AF.Exp, scale=-0.5), r=[("stat", c)], w=[("stat", c)])
        S.op("dve", lambda e: e.tensor_scalar(hb[i][0:ntok, :], xt[i][0:ntok, :], ss, None, ALU.mult),
             r=[("xt", i), ("stat", c)], w=[("hb", i)])
        for g4 in range(4):
            b = rbank()
            pv = banks[b][:, :].bitcast(BF16)
            for j in range(4):
                kc = g4 * 4 + j
                S.op("pe", lambda e, o=pv[:, j * 128:j * 128 + ntok], a=hb[i][0:ntok, kc * 128:(kc + 1) * 128]:
                     e.transpose(o, a, ident_b[0:ntok, 0:ntok]),
                     r=[("hb", i), "ident_b"], w=[("bank", b)])
            for j in range(4):
                kc = g4 * 4 + j
                if g4 % 2 == 0:
                    S.op("act", lambda e, o=hT[:, kc, col0:col0 + ntok], a=pv[:, j * 128:j * 128 + ntok], sc=gT[:, kc:kc + 1]:
                         e.activation(o, a, AF.Copy, scale=sc),
                         r=[("bank", b), "gmixT", "gffnT"], w=[hT_key])
                else:
                    S.op("dve", lambda e, o=hT[:, kc, col0:col0 + ntok], a=pv[:, j * 128:j * 128 + ntok], sc=gT[:, kc:kc + 1]:
                         e.tensor_scalar(o, a, sc, None, ALU.mult),
                         r=[("bank", b), "gmixT", "gffnT"], w=[hT_key])

    C.RB = 5

    def rbank():
        b = C.bank_rr
        C.bank_rr = (C.bank_rr + 1) % C.RB
        return b

    def proj_tm(wv, wkey, kcs, ncols, src, skey_fn, tiles, evac, after_tile=None):
        for ti, (col0, ntok, tag) in enumerate(tiles):
            b = rbank()
            pt = banks[b][0:ntok, 0:ncols]
            for kc in range(kcs):
                S.op("pe", lambda e, o=pt, l=src[:, kc, col0:col0 + ntok], rr=wv[:, kc, :], st=(kc == 0), sp=(kc == kcs - 1):
                     e.matmul(o, l, rr, start=st, stop=sp), r=[wkey, skey_fn(ti)], w=[("bank", b)])
            evac(ti, tag, pt, ("bank", b))
            if after_tile is not None:
                after_tile(ti)

    def proj_fm(wv, wkey, kcs, c0, m, src, skeys, n0, n, evac):
        b = rbank()
        pt = banks[b][0:m, 0:n]
        for kc in range(kcs):
            S.op("pe", lambda e, o=pt, l=wv[:, kc, c0:c0 + m], rr=src[:, kc, n0:n0 + n], st=(kc == 0), sp=(kc == kcs - 1):
                 e.matmul(o, l, rr, start=st, stop=sp), r=[wkey] + list(skeys), w=[("bank", b)])
        evac(pt, ("bank", b))

    def act_rstd(out_ap, ss_ap, n, rkeys, wkeys):
        S.op("act", lambda e: e.activation(out_ap, ss_ap, AF.Ln, bias=EPS, scale=1.0 / n), r=rkeys, w=wkeys)
        S.op("act", lambda e: e.activation(out_ap, out_ap, AF.Exp, scale=-0.5), r=wkeys, w=wkeys)

    es1 = new_scope()
    alloc_wbufs(3, 8192)
    ckvr = sb("ckvr", [128, 8, 512], F32)
    ckvb = [sb("ckvb%d" % i, [128, 512], BF16) for i in range(2)]
    ckvT = sb("ckvT", [128, 4, 1024], BF16)
    krw = sb("krw", [128, 8, 80], F32)
    krr = sb("krr", [128, 8, 64], F32)
    krtmp = sb("krtmp", [128, 4, 32], F32)
    krb = [sb("krb%d" % i, [128, 64], BF16) for i in range(2)]
    krT = sb("krT", [64, 1024], BF16)
    krss = sb("krss", [128, 8], F32)
    ropc = sb("ropc", [128, 8, 64], F32)
    alrT = sb("alrT", [17, 1024], F32)
    lnl2 = [hb[0][:, :].bitcast(F32), xt[0][:, 0:1024]]
    Ebuf = sb("Ebuf", [128, 8, 1024], BF16)
    Dt = sb("Dt", [128, 8, 8], F32)
    kend = sb("kend", [128, 8, 256], BF16)
    vh = sb("vh", [128, 8, 512], BF16)
    Sst = sb("Sst", [128, 8, 512], F32)
    Sbf = sb("Sbf", [128, 2, 512], BF16)
    knT = [sb("knT%d" % i, [128, 1024], BF16) for i in range(2)]
    sqb = [sb("sqb%d" % i, [128, 512], BF16) for i in range(3)]
    vt = vh
    invk = sb("invk", [128, 128], F32)
    qkT = sb("qkT", [128, 4, 128], BF16)
    eqk = sb("eqk", [128, 4, 128], F32)
    ATb = sb("ATb", [128, 128], BF16)
    onb = sb("onb", [128, 512], BF16)
    ogT = sb("ogT", [128, 16, 128], BF16)
    cumT = xt[0][:, 1024:2048].rearrange("p (a b) -> p a b", a=8)

    def tile_keys(prefix, n=8):
        return [(prefix, i) for i in range(n)]

    def rope_tm(dst, src, cs, ntok, rk, wk):
        x1, x2 = src[:, 0:32], src[:, 32:64]
        co, si = cs[:, 0:32], cs[:, 32:64]
        t = krtmp
        S.op("dve", lambda e: e.tensor_tensor(t[0:ntok, 0, :], x1, co, ALU.mult), r=rk, w=["krtmp"])
        S.op("dve", lambda e: e.tensor_tensor(t[0:ntok, 1, :], x2, si, ALU.mult), r=rk, w=["krtmp"])
        S.op("dve", lambda e: e.tensor_tensor(t[0:ntok, 2, :], x2, co, ALU.mult), r=rk, w=["krtmp"])
        S.op("dve", lambda e: e.tensor_tensor(t[0:ntok, 3, :], x1, si, ALU.mult), r=rk, w=["krtmp"])
        S.op("dve", lambda e: e.tensor_tensor(dst[:, 0:32], t[0:ntok, 0, :], t[0:ntok, 1, :], ALU.subtract), r=["krtmp"], w=wk)
        S.op("dve", lambda e: e.tensor_tensor(dst[:, 32:64], t[0:ntok, 2, :], t[0:ntok, 3, :], ALU.add), r=["krtmp"], w=wk)

    def kv_finish(tiles, dst, kt0, fill=None):
        fill = fill if fill is not None else []

        def pumpf():
            if fill:
                fill.pop(0)()
        n = len(tiles)
        ncols = n * 128
        for idx, (i, ntok) in enumerate(tiles):
            j = idx % 2
            if ntok < 128:
                S.op("dve", lambda e, j=j: e.memset(ckvb[j][:, :], 0.0), w=[("ckvb", j)])
                S.op("dve", lambda e, j=j: e.memset(krb[j][:, :], 0.0), w=[("krb", j)])
            S.op("act", lambda e, j=j, i=i, ntok=ntok: e.activation(ckvb[j][0:ntok, :], ckvr[0:ntok, i, :], AF.Copy),
                 r=[("ckvr", i)], w=[("ckvb", j)])
            S.op("dve", lambda e, j=j, i=i, ntok=ntok: e.tensor_copy(krb[j][0:ntok, :], krr[0:ntok, i, :]),
                 r=[("krr", i)], w=[("krb", j)])
            S.op("act", lambda e, i=i, ntok=ntok: e.activation(junk[0:ntok, 0:64], krr[0:ntok, i, :], AF.Square,
                                                              accum_out=krss[0:ntok, i:i + 1]),
                 r=[("krr", i)], w=[("krss", i), "junk"])
            b = rbank()
            pv = banks[b][:, :].bitcast(BF16)
            for kc in range(4):
                S.op("pe", lambda e, kc=kc, j=j, pv=pv: e.transpose(pv[:, kc * 128:(kc + 1) * 128], ckvb[j][:, kc * 128:(kc + 1) * 128], ident_b[:, :]),
                     r=[("ckvb", j), "ident_b"], w=[("bank", b)])
            b2 = rbank()
            pv2 = banks[b2][:, :].bitcast(BF16)
            S.op("pe", lambda e, j=j, pv2=pv2: e.transpose(pv2[0:64, 0:128], krb[j][:, :], ident_b[:, :]),
                 r=[("krb", j), "ident_b"], w=[("bank", b2)])
            if "skipB" not in KSKIP:
                S.op("act", lambda e, pv=pv, idx=idx: e.activation(
                    ckvT[:, :, idx * 128:(idx + 1) * 128], pv[:, 0:512].rearrange("p (k c) -> p k c", k=4), AF.Copy),
                    r=[("bank", b)], w=[("ckvT", idx)])
            S.op("dve", lambda e, pv2=pv2, idx=idx: e.tensor_copy(krT[:, idx * 128:(idx + 1) * 128], pv2[0:64, 0:128]),
                 r=[("bank", b2)], w=[("krT", idx)])
        if "var2" in KSKIP:
            dma("act", y_p[0:64, 0:ncols // 2].bitcast(BF16), krT[:, 0:ncols], [("krT", i) for i in range(n)], [("scr_krT")], "krT_o")
        elif "var1" in KSKIP:
            dma("act", dst["krT"][:, kt0 * 128:kt0 * 128 + ncols], hb[0][0:64, 0:ncols], [("hb", 0)], [("scr_krT")], "krT_o")
        else:
            dma("act", dst["krT"][:, kt0 * 128:kt0 * 128 + ncols], krT[:, 0:ncols], [("krT", i) for i in range(n)], [("scr_krT")], "krT_o")
        if "cut1" in KSKIP:
            return
        halves = [(c0, min(512, ncols - c0)) for c0 in range(0, ncols, 512)]
        pend_ssk = []
        for sl in range(4):
            wv, wkey = load_slab(w_ukv[:, sl * 1024:(sl + 1) * 1024], 4, 1024)
            for hh in range(4):
                h = sl * 4 + hh
                kb = h % 2
                if hh % 2 == 0:
                    pumpf()
                for (c0, cn) in halves:
                    b = rbank()
                    pt = banks[b][:, 0:cn]
                    for kc in range(4):
                        S.op("pe", lambda e, pt=pt, kc=kc, hh=hh, c0=c0, cn=cn, wv=wv: e.matmul(
                            pt, wv[:, kc, hh * 256:hh * 256 + 128], ckvT[:, kc, c0:c0 + cn], start=(kc == 0), stop=(kc == 3)),
                            r=[wkey] + [("ckvT", t) for t in range(c0 // 128, (c0 + cn) // 128)], w=[("bank", b)])
                    S.op("dve", lambda e, pt=pt, kb=kb, c0=c0, cn=cn: e.tensor_copy(knT[kb][:, c0:c0 + cn], pt),
                         r=[("bank", b)], w=[("knT", kb)])
                    sj = C.sq_rr = (getattr(C, "sq_rr", 0) + 1) % 3
                    S.op("act", lambda e, kb=kb, sj=sj, c0=c0, cn=cn: e.activation(sqb[sj][:, 0:cn], knT[kb][:, c0:c0 + cn], AF.Square),
                         r=[("knT", kb)], w=[("sqb", sj)])
                    def ssk_mm(sj=sj, c0=c0, cn=cn, h=h):
                        for t in range(cn // 128):
                            idx = c0 // 128 + t
                            S.op("pe", lambda e, t=t, idx=idx: e.matmul(
                                banks[7][:, idx * 16 + h:idx * 16 + h + 1], sqb[sj][:, t * 128:(t + 1) * 128], ones_b[:, 0:1],
                                start=True, stop=True), r=[("sqb", sj), "ones_b"], w=["ssk"])
                    if pend_ssk:
                        pend_ssk.pop(0)()
                    pend_ssk.append(ssk_mm)
                dma("act", dst["knT"][h, :, kt0 * 128:kt0 * 128 + ncols], knT[kb][:, 0:ncols], [("knT", kb)], ["scr_knT"], ("knT_o", kb))
            for idx in range(n if "skipD" not in KSKIP else 0):
                b = rbank()
                pt = banks[b][:, 0:512]
                for kc in range(4):
                    rhs = wv[:, kc, :].rearrange("p (h c) -> p h c", h=4)[:, :, 128:256]
                    S.op("pe", lambda e, pt=pt, kc=kc, idx=idx, rhs=rhs: e.matmul(
                        pt.rearrange("p (h c) -> p h c", h=4), ckvT[:, kc, idx * 128:(idx + 1) * 128], rhs, start=(kc == 0), stop=(kc == 3)),
                        r=[wkey, ("ckvT", idx)], w=[("bank", b)])
                eng = "act" if idx % 2 else "dve"
                if eng == "act":
                    S.op("act", lambda e, pt=pt, idx=idx: e.activation(vt[:, idx, :], pt, AF.Copy), r=[("bank", b)], w=["vh"])
                else:
                    S.op("dve", lambda e, pt=pt, idx=idx: e.tensor_copy(vt[:, idx, :], pt), r=[("bank", b)], w=["vh"])
            for hh in range(4):
                dma("act", dst["v"][sl * 4 + hh, :, kt0:kt0 + n, :], vt[:, 0:n, hh * 128:(hh + 1) * 128], ["vh"], ["scr_v"], ("vt_o", hh))
        while pend_ssk:
            pend_ssk.pop(0)()
        while fill:
            pumpf()
        for idx, (i, ntok) in enumerate(tiles):
            S.op("dve", lambda e, idx=idx, i=i: e.tensor_scalar(invk[:, idx * 16:(idx + 1) * 16], banks[7][:, idx * 16:(idx + 1) * 16],
                                                                 krss[:, i:i + 1], 1.0 / 192.0, ALU.add, ALU.mult),
                 r=["ssk", ("krss", i)], w=["invk"])
        act_rstd(invk[:, 0:n * 16], invk[:, 0:n * 16], 1.0, ["invk"], ["invk"])
        if "var5" in KSKIP:
            S.op("dve", lambda e: e.tensor_copy(stat[:, 0:16], invk[:, 0:16]), r=["invk"], w=[("stat", 0)])
        elif "var6" in KSKIP:
            S.op("pool", lambda e: e.tensor_copy(hb[0][:, 0:128].bitcast(F32), invk[:, 0:64]), r=["invk"], w=[("hb", 0)])
            dma("act", kr_p[0:128, :], hb[0][:, 0:128].bitcast(F32), [("hb", 0)], [], "zz")
        elif "var3" in KSKIP:
            dma("sp" if "var3sp" in KSKIP else "act", kr_p[0:128, :], invk[:, 0:64] if "var3hb" not in KSKIP else hb[0][:, 0:128].bitcast(F32), ["invk"], [], "zz")
        elif "var4" in KSKIP:
            dma("act", dst["invk"][:, kt0:kt0 + n, :], invk[:, 0:n * 16].rearrange("p (j h) -> p j h", h=16), ["invk"], [], "zz")
        else:
            dma("act", dst["invk"][:, kt0:kt0 + n, :], invk[:, 0:n * 16].rearrange("p (j h) -> p j h", h=16), ["invk"], ["scr_invk"], "invk_o")


    knT_s = dout("knT_s", [2, 16, 128, 17 * 128], BF16)
    krT_s = dout("krT_s", [2, 64, 17 * 128], BF16)
    v_s = dout("v_s", [2, 16, 128, 17, 128], BF16)
    invk_s = dout("invk_s", [2, 128, 17, 16], F32)
    ogla_s = dout("ogla_s", [2, 128, 16, 16], BF16)
    S.op("pool", lambda e: e.memset(alrT[:, :], 1.0), w=["alrT"])
    S.op("pool", lambda e: e.memset(krss[:, :], 1.0), w=[("krss", i) for i in range(8)])
    S.op("pool", lambda e: e.memset(Sst[:, :, :], 0.0), w=["Sst"])

    def evac_copy(eng, out, pt, bkey, wkeys):
        if eng == "act":
            S.op("act", lambda e: e.activation(out, pt, AF.Copy), r=[bkey], w=wkeys)
        else:
            S.op("dve", lambda e: e.tensor_copy(out, pt), r=[bkey], w=wkeys)

    def front_round(tiles, rope_src, own, outs, ogdst):
        ptiles = [(c0, nt, i) for (i, c0, nt) in tiles]
        hk = lambda ti: ("hT", tiles[ti][0])
        allhk = [("hT", i) for (i, _, _) in tiles]
        ntot = sum(nt for (_, _, nt) in tiles)
        cbase = tiles[0][1]
        for (i, c0, nt) in tiles:
            dma("sp", ropc[0:nt, i, :], rope_src(i, nt), [], [("ropc", i)], ("ropc", i % 2))
        wv, wkey = load_slab(w_in[:, C_ALR:C_ALR + 16], 16, 16)
        for n0 in range(0, ntot, 512):
            n = min(512, ntot - n0)
            proj_fm(wv, wkey, 16, 0, 16, hT, allhk, cbase + n0, n,
                    lambda pt, bk, n0=n0, n=n: evac_copy("act", alrT[0:16, n0:n0 + n], pt, bk, ["alrT"]))
        stages = []
        LK = [("hb", 0), ("xt", 0)]
        for ti, (i, c0, nt) in enumerate(tiles):
            a0 = c0 - cbase

            def stA(i=i, nt=nt, a0=a0):
                for half in range(2):
                    b = rbank()
                    pt = banks[b][0:nt, :]
                    S.op("pe", lambda e, pt=pt, half=half: e.matmul(pt, alrT[0:17, a0:a0 + nt], wa2_f[0:17, half * 512:(half + 1) * 512], start=True, stop=True),
                         r=["alrT", "wa2_f"], w=[("bank", b)])
                    S.op("act", lambda e, pt=pt, half=half: e.activation(lnl2[i % 2][0:nt, half * 512:(half + 1) * 512], pt, AF.Exp, scale=-1.0),
                         r=[("bank", b)], w=[LK[i % 2]])
                S.op("act", lambda e: e.activation(lnl2[i % 2][0:nt, :], lnl2[i % 2][0:nt, :], AF.Ln, bias=1.0), r=[LK[i % 2]], w=[LK[i % 2]])

            def stB(i=i, nt=nt):
                for half in range(2):
                    b = rbank()
                    pt = banks[b][0:nt, :]
                    S.op("pe", lambda e, pt=pt, half=half: e.matmul(pt, aft_f[0:nt, 0:nt], lnl2[i % 2][0:nt, half * 512:(half + 1) * 512], start=True, stop=True),
                         r=[LK[i % 2], "cst"], w=[("bank", b)])
                    S.op("act", lambda e, pt=pt, half=half: e.activation(Ebuf[0:nt, i, half * 512:(half + 1) * 512], pt, AF.Exp, scale=-1.0 / 16),
                         r=[("bank", b)], w=[("Ebuf", i)])

            def stC(i=i, nt=nt, ti=ti):
                db = 5 + (ti % 2)
                for ch in range(8):
                    S.op("pe", lambda e, ch=ch: e.matmul(banks[db][:, ch:ch + 1], lnl2[i % 2][0:nt, ch * 128:(ch + 1) * 128], ones_f[0:nt, 0:1], start=True, stop=True),
                         r=[LK[i % 2], "ones_f"], w=[("bank", db)])
                S.op("act", lambda e: e.activation(Dt[:, i, :], banks[db][:, 0:8], AF.Exp, scale=-1.0 / 16), r=[("bank", db)], w=[("Dt", i)])
                if own == i:
                    for ch in range(8):
                        b = rbank()
                        pt = banks[b][:, 0:nt]
                        S.op("pe", lambda e, pt=pt, ch=ch: e.matmul(pt, lnl2[i % 2][0:nt, ch * 128:(ch + 1) * 128], tri_f[0:nt, 0:nt], start=True, stop=True),
                             r=[LK[i % 2], "cst"], w=[("bank", b)])
                        S.op("act", lambda e, pt=pt, ch=ch: e.activation(cumT[:, ch, 0:nt], pt, AF.Copy), r=[("bank", b)], w=[("xt", 0)])
            stages += [stA, stB, stC]
        if "gla" in KSKIP:
            stages = []

        def pump(n=1):
            for _ in range(n):
                if stages:
                    stages.pop(0)()
        wv, wkey = load_slab(w_in[:, C_KR:C_KR + 64], 16, 64)

        def ev_kr(ti, i, pt, bk):
            nt = tiles[ti][2]
            evac_copy("act", krw[0:nt, i, 0:64], pt, bk, [("krw", i)])
            rope_tm(krr[0:nt, i, :], krw[0:nt, i, 0:64], ropc[0:nt, i, :], nt, [("krw", i), ("ropc", i)], [("krr", i)])
        proj_tm(wv, wkey, 16, 64, hT, hk, ptiles, ev_kr, after_tile=lambda ti: pump(1))
        wv, wkey = load_slab(w_in[:, C_KV:C_KV + 512], 16, 512)

        def ev_ckv(ti, i, pt, bk):
            nt = tiles[ti][2]
            evac_copy("act" if ti % 2 else "dve", ckvr[0:nt, i, :], pt, bk, [("ckvr", i)])
        proj_tm(wv, wkey, 16, 512, hT, hk, ptiles, ev_ckv, after_tile=lambda ti: pump(2))
        for ti, (i, c0, nt) in enumerate(tiles):
            c = statcol()
            ss = stat[0:nt, c:c + 1]
            S.op("act", lambda e, i=i, nt=nt, ss=ss: e.activation(ckvb[0][0:nt, :], ckvr[0:nt, i, :], AF.Square, accum_out=ss),
                 r=[("ckvr", i)], w=[("stat", c), ("ckvb", 0)])
            act_rstd(ss, ss, 512.0, [("stat", c)], [("stat", c)])
            S.op("dve", lambda e, i=i, nt=nt, ss=ss: e.scalar_tensor_tensor(ckvr[0:nt, i, :], ckvr[0:nt, i, :], ss, gkv_bc[0:nt, :], ALU.mult, ALU.mult),
                 r=[("ckvr", i), ("stat", c), "gkv_bc"], w=[("ckvr", i)])
        if own is not None:
            nt = [t for t in tiles if t[0] == own][0][2]
            dma("act", outs[0], ckvr[0:nt, own, :], [("ckvr", own)], [], ("ckvr_o", own))
            dma("act", outs[1], krr[0:nt, own, :], [("krr", own)], [], ("krr_o", own))
        pump(len(stages))
        if "gla" in KSKIP:
            return
        for h in range(4):
            wvk, wkk = load_slab(w_in[:, C_GK + h * 256:C_GK + (h + 1) * 256], 16, 256)

            def ev_k(ti, i, pt, bk, h=h):
                nt = tiles[ti][2]
                S.op("dve", lambda e: e.tensor_tensor(kend[0:nt, i, :], pt, Ebuf[0:nt, i, h * 256:(h + 1) * 256], ALU.mult),
                     r=[bk, ("Ebuf", i)], w=[("kend", i)])
            proj_tm(wvk, wkk, 16, 256, hT, hk, ptiles, ev_k)
            if own is not None:
                oi, oc0, ont = [t for t in tiles if t[0] == own][0]
                for c in range(2):
                    S.op("act", lambda e, c=c, h=h, ont=ont: e.activation(eqk[:, c * 2, 0:ont], cumT[:, h * 2 + c, 0:ont], AF.Exp, scale=-1.0 / 16), r=[("xt", 0)], w=["eqk"])
                    S.op("act", lambda e, c=c, h=h, ont=ont: e.activation(eqk[:, c * 2 + 1, 0:ont], cumT[:, h * 2 + c, 0:ont], AF.Exp, scale=1.0 / 16), r=[("xt", 0)], w=["eqk"])
                    proj_fm(wvk, wkk, 16, c * 128, 128, hT, [("hT", own)], oc0, ont,
                            lambda pt, bk, c=c, ont=ont: S.op("dve", lambda e: e.tensor_tensor(qkT[:, 2 + c, 0:ont], pt, eqk[:, c * 2 + 1, 0:ont], ALU.mult), r=[bk, "eqk"], w=["qkT"]))
            wv, wkey = load_slab(w_in[:, C_GV + h * 512:C_GV + (h + 1) * 512], 16, 512)

            def ev_v(ti, i, pt, bk):
                nt = tiles[ti][2]
                evac_copy("act" if ti % 2 else "dve", vh[0:nt, i, :], pt, bk, ["vh"])
            proj_tm(wv, wkey, 16, 512, hT, hk, ptiles, ev_v)
            if own is not None:
                wv, wkey = load_slab(w_in[:, C_GQ + h * 256:C_GQ + (h + 1) * 256], 16, 256)
                for c in range(2):
                    proj_fm(wv, wkey, 16, c * 128, 128, hT, [("hT", own)], oc0, ont,
                            lambda pt, bk, c=c, ont=ont: S.op("dve", lambda e: e.scalar_tensor_tensor(qkT[:, c, 0:ont], pt, 1.0 / 16, eqk[:, c * 2, 0:ont], ALU.mult, ALU.mult), r=[bk, "eqk"], w=["qkT"]))
            for ti, (i, c0, nt) in enumerate(tiles):
                if own == i:
                    S.op("act", lambda e, h=h: e.activation(Sbf[:, :, :], Sst[:, h * 2:h * 2 + 2, :], AF.Copy), r=["Sst"], w=["Sbf"])
                    b = rbank()
                    pa = banks[b][0:nt, 0:nt]
                    for c in range(2):
                        S.op("pe", lambda e, pa=pa, c=c, nt=nt: e.matmul(pa, qkT[:, 2 + c, 0:nt], qkT[:, c, 0:nt], start=(c == 0), stop=(c == 1)), r=["qkT"], w=[("bank", b)])
                    S.op("dve", lambda e, pa=pa, nt=nt: e.tensor_tensor(ATb[0:nt, 0:nt], pa, tri_f[0:nt, 0:nt], ALU.mult), r=[("bank", b), "cst"], w=["ATb"])
                    b = rbank()
                    po = banks[b][0:nt, :]
                    S.op("pe", lambda e, po=po, nt=nt, i=i: e.matmul(po, ATb[0:nt, 0:nt], vh[0:nt, i, :], start=True, stop=False), r=["ATb", "vh"], w=[("bank", b)])
                    for c in range(2):
                        S.op("pe", lambda e, po=po, c=c, nt=nt: e.matmul(po, qkT[:, c, 0:nt], Sbf[:, c, :], start=False, stop=(c == 1)), r=["qkT", "Sbf"], w=[("bank", b)])
                    cc = statcol()
                    ss = stat[0:nt, cc:cc + 1]
                    S.op("act", lambda e, po=po, nt=nt, ss=ss: e.activation(onb[0:nt, :], po, AF.Square, accum_out=ss), r=[("bank", b)], w=[("stat", cc), "onb"])
                    act_rstd(ss, ss, 512.0, [("stat", cc)], [("stat", cc)])
                    S.op("dve", lambda e, po=po, nt=nt, ss=ss: e.scalar_tensor_tensor(onb[0:nt, :], po, ss, ggla_bc[0:nt, :], ALU.mult, ALU.mult),
                         r=[("bank", b), ("stat", cc), "ggla_bc"], w=["onb"])
                    b2 = rbank()
                    pv = banks[b2][:, :].bitcast(BF16)
                    for q4 in range(4):
                        S.op("pe", lambda e, pv=pv, q4=q4, nt=nt: e.transpose(pv[:, q4 * 128:q4 * 128 + nt], onb[0:nt, q4 * 128:(q4 + 1) * 128], ident_b[0:nt, 0:nt]),
                             r=["onb", "ident_b"], w=[("bank", b2)])
                    S.op("act", lambda e, pv=pv, h=h, nt=nt: e.activation(ogT[:, h * 4:(h + 1) * 4, 0:nt], pv[:, 0:512].rearrange("p (q c) -> p q c", q=4)[:, :, 0:nt], AF.Copy),
                         r=[("bank", b2)], w=["ogT"])
                for c in range(2):
                    b = rbank()
                    pt = banks[b][:, :]
                    S.op("pe", lambda e, pt=pt, nt=nt, i=i, c=c: e.matmul(pt, kend[0:nt, i, c * 128:(c + 1) * 128], vh[0:nt, i, :], start=True, stop=True),
                         r=[("kend", i), "vh"], w=[("bank", b)])
                    S.op("dve", lambda e, pt=pt, i=i, hc=h * 2 + c: e.scalar_tensor_tensor(Sst[:, hc, :], Sst[:, hc, :], Dt[:, i, hc:hc + 1], pt, ALU.mult, ALU.add),
                         r=[("bank", b), ("Dt", i), "Sst"], w=["Sst"])
        if own is not None:
            dma("act", ogdst, ogT[:, :, 0:ont], ["ogT"], [], "ogT_o")


    dstp = dict(knT=knT_scr, krT=krT_scr, v=v_scr, invk=invk_scr)
    full8 = [(i, i * 128, 128) for i in range(8)]
    def nt_closure(r, i):
        t = 8 * r + i
        return lambda: norm_transpose(xs[t * 128:(t + 1) * 128, :], 128, gmixT, i * 128, ("hT", i))
    for r in range(KR if stage >= 1 else 0):
        if r == 0:
            for i in range(8):
                nt_closure(0, i)()
        if "nofront" not in KSKIP:
            front_round(full8, lambda i, nt, r=r: ropet[(8 * r + i) * 128:(8 * r + i) * 128 + nt, :], 7,
                    (ckv_p[r * 128:(r + 1) * 128, :], kr_p[r * 128:(r + 1) * 128, :]), ogla_scr[r])
        if "nokv" not in KSKIP.split(","):
            kv_finish([(i, 128) for i in range(8)], dstp, 8 * r,
                      fill=[nt_closure(r + 1, i) for i in range(8)] if r + 1 < KR else None)
    dma("act", st_p.rearrange("h (c p) v -> p (h c) v", p=128), Sst[:, :, :], ["Sst"], [], "Sst_o")
    for s_ in range(2 if stage >= 2 else 0):
        dsts = dict(knT=knT_s[s_], krT=krT_s[s_], v=v_s[s_], invk=invk_s[s_])
        for rr in range(2):
            for i in range(8):
                t = 8 * rr + i
                dma("sp", ckvr[:, i, :], cckv[s_, t * 128:(t + 1) * 128, :], [], [("ckvr", i)], ("ckvr_i", i % 2))
                dma("sp", krr[:, i, :], ckr[s_, t * 128:(t + 1) * 128, :], [], [("krr", i)], ("krr_i", i % 2))
            kv_finish([(i, 128) for i in range(8)], dsts, 8 * rr)
        norm_transpose(x_s[s_ * 16:(s_ + 1) * 16, :], 16, gmixT, 0, ("hT", 0))
        dma("sp", Sst[:, :, :], st_in[s_].rearrange("h (c p) v -> p (h c) v", p=128), [], ["Sst"], "Sst_i")
        front_round([(0, 0, 16)], lambda i, nt, s_=s_: ropes[s_ * 16:s_ * 16 + nt, :], 0,
                    (ckv_s[s_ * 16:(s_ + 1) * 16, :], kr_s[s_ * 16:(s_ + 1) * 16, :]), ogla_s[s_])
        kv_finish([(0, 16)], dsts, 16)
        dma("act", st_s[s_].rearrange("h (c p) v -> p (h c) v", p=128), Sst[:, :, :], ["Sst"], [], "Sst_o")


    if stage >= 3:
        S.barrier(NOPS)
        es1.close()
        es2 = new_scope()
        C.bank_rr = 0
        SCALE = 192.0 ** -0.5
        oT = sb("oT", [128, 16, 1056], BF16)
        QrT = sb("QrT", [128, 8, 1056], BF16)
        gqk = sb("gqk", [128, 192], F32)
        gtmp = sb("gtmp", [128, 192], F32)
        gq_bc = sb("gq_bc", [128, 512], F32)
        rope2 = sb("rope2", [128, 9, 64], F32)
        rtmp = sb("rtmp", [128, 4, 32], F32)
        tiles2 = [(ti, ti * 128, 128) for ti in range(8)] + [(8, 1024, 32)]
        dma("sp", gq_bc[:, :], g_q.partition_broadcast(128), [], ["gq_bc"], "c0")
        dma("sp", gqk[:, 0:128], gq_n.partition_broadcast(128), [], ["gqk"], "c0")
        dma("sp", gqk[:, 128:160], gq_r.partition_broadcast(128), [], ["gqk"], "c0")
        dma("sp", gtmp[:, 0:128], gk_n.partition_broadcast(128), [], ["gtmp"], "c0")
        dma("sp", gtmp[:, 128:160], gk_r.partition_broadcast(128), [], ["gtmp"], "c0")
        S.op("dve", lambda e: e.scalar_tensor_tensor(gqk[:, 0:160], gqk[:, 0:160], SCALE, gtmp[:, 0:160], ALU.mult, ALU.mult), r=["gqk", "gtmp"], w=["gqk"])
        S.op("dve", lambda e: e.tensor_copy(gqk[:, 160:192], gqk[:, 128:160]), r=["gqk"], w=["gqk"])
        for k in range(8):
            t = 8 * k + 7
            dma("sp", rope2[:, k, :], ropet[t * 128:(t + 1) * 128, :], [], [("rope2", k)], "c0")
        dma("sp", rope2[0:32, 8, :], ropes, [], [("rope2", 8)], "c0")

        def own_norm_transpose():
            for k in range(8):
                t = 8 * k + 7
                norm_transpose(xs[t * 128:(t + 1) * 128, :], 128, gmixT, k * 128, ("hT", k))
            norm_transpose(x_s, 32, gmixT, 1024, ("hT", 8))

        def rope_ip(x, cs, ntok, rk, wk):
            x1, x2 = x[:, 0:32], x[:, 32:64]
            co, si = cs[:, 0:32], cs[:, 32:64]
            t = rtmp
            S.op("dve", lambda e: e.tensor_tensor(t[0:ntok, 0, :], x1, co, ALU.mult), r=rk, w=["rtmp"])
            S.op("dve", lambda e: e.tensor_tensor(t[0:ntok, 1, :], x2, si, ALU.mult), r=rk, w=["rtmp"])
            S.op("dve", lambda e: e.tensor_tensor(t[0:ntok, 2, :], x2, co, ALU.mult), r=rk, w=["rtmp"])
            S.op("dve", lambda e: e.tensor_tensor(t[0:ntok, 3, :], x1, si, ALU.mult), r=rk, w=["rtmp"])
            S.op("dve", lambda e: e.tensor_tensor(x1, t[0:ntok, 0, :], t[0:ntok, 1, :], ALU.subtract), r=["rtmp"], w=wk)
            S.op("dve", lambda e: e.tensor_tensor(x2, t[0:ntok, 2, :], t[0:ntok, 3, :], ALU.add), r=["rtmp"], w=wk)

        own_norm_transpose()
        hk2 = lambda ti: ("hT", ti)
        ptiles2 = [(c0, nt, ti) for (ti, c0, nt) in tiles2]
        esq = new_scope()
        alloc_wbufs()
        qlr = sb("qlr", [128, 9, 512], BF16)
        qlb = sb("qlb", [128, 512], BF16)
        qlT = sb("qlT", [128, 4, 1056], BF16)
        qh2 = sb("qh2", [128, 2, 192], F32)
        qnb2 = sb("qnb2", [128, 2, 128], BF16)
        qrb = sb("qrb", [128, 128], BF16)
        for half in range(2):
            wv, wkey = load_slab(w_in[:, C_QLAT + half * 256:C_QLAT + (half + 1) * 256], 16, 256)

            def ev_ql(ti, tag, pt, bk, half=half):
                nt = tiles2[ti][2]
                evac_copy("act" if ti % 2 else "dve", qlr[0:nt, ti, half * 256:(half + 1) * 256], pt, bk, [("qlr", ti)])
            proj_tm(wv, wkey, 16, 256, hT, hk2, ptiles2, ev_ql)
        for (ti, c0, nt) in tiles2:
            c = statcol()
            ss = stat[0:nt, c:c + 1]
            S.op("act", lambda e, ti=ti, nt=nt, ss=ss: e.activation(qlb[0:nt, :], qlr[0:nt, ti, :], AF.Square, accum_out=ss), r=[("qlr", ti)], w=[("stat", c), "qlb"])
            act_rstd(ss, ss, 512.0, [("stat", c)], [("stat", c)])
            S.op("dve", lambda e, ti=ti, nt=nt, ss=ss: e.scalar_tensor_tensor(qlb[0:nt, :], qlr[0:nt, ti, :], ss, gq_bc[0:nt, :], ALU.mult, ALU.mult),
                 r=[("qlr", ti), ("stat", c), "gq_bc"], w=["qlb"])
            b = rbank()
            pv = banks[b][:, :].bitcast(BF16)
            for kc in range(4):
                S.op("pe", lambda e, pv=pv, kc=kc, nt=nt: e.transpose(pv[:, kc * 128:kc * 128 + nt], qlb[0:nt, kc * 128:(kc + 1) * 128], ident_b[0:nt, 0:nt]),
                     r=["qlb", "ident_b"], w=[("bank", b)])
            S.op("act", lambda e, pv=pv, c0=c0, nt=nt: e.activation(qlT[:, :, c0:c0 + nt], pv[:, 0:512].rearrange("p (k c) -> p k c", k=4)[:, :, 0:nt], AF.Copy),
                 r=[("bank", b)], w=[("qlT", ti)])
        for sl in range(4):
            wv, wkey = load_slab(w_uq[:, sl * 768:(sl + 1) * 768], 4, 768)
            for (ti, c0, nt) in tiles2:
                for g in range(2):
                    b = rbank()
                    pt = banks[b][0:nt, 0:384]
                    for kc in range(4):
                        S.op("pe", lambda e, pt=pt, kc=kc, c0=c0, nt=nt, g=g, wv=wv: e.matmul(pt, qlT[:, kc, c0:c0 + nt], wv[:, kc, g * 384:(g + 1) * 384], start=(kc == 0), stop=(kc == 3)),
                             r=[wkey, ("qlT", ti)], w=[("bank", b)])
                    S.op("act", lambda e, pt=pt, nt=nt: e.activation(qh2[0:nt, :, :], pt.rearrange("p (a c) -> p a c", a=2), AF.Copy), r=[("bank", b)], w=["qh2"])
                    for hh in range(2):
                        rope_ip(qh2[0:nt, hh, 128:192], rope2[0:nt, ti, :], nt, ["qh2", ("rope2", ti)], ["qh2"])
                        c = statcol()
                        ss = stat[0:nt, c:c + 1]
                        S.op("act", lambda e, nt=nt, hh=hh, ss=ss: e.activation(qlb[0:nt, 0:192], qh2[0:nt, hh, :], AF.Square, accum_out=ss), r=["qh2"], w=[("stat", c), "qlb"])
                        act_rstd(ss, ss, 192.0, [("stat", c)], [("stat", c)])
                        S.op("dve", lambda e, nt=nt, hh=hh, ss=ss: e.scalar_tensor_tensor(qnb2[0:nt, hh, :], qh2[0:nt, hh, 0:128], ss, gqk[0:nt, 0:128], ALU.mult, ALU.mult),
                             r=["qh2", ("stat", c), "gqk"], w=["qhb2"])
                        S.op("dve", lambda e, nt=nt, hh=hh, ss=ss: e.scalar_tensor_tensor(qrb[0:nt, hh * 64:(hh + 1) * 64], qh2[0:nt, hh, 128:192], ss, gqk[0:nt, 128:192], ALU.mult, ALU.mult),
                             r=["qh2", ("stat", c), "gqk"], w=["qhb2"])
                        h = sl * 4 + g * 2 + hh
                        bA = rbank()
                        pvA = banks[bA][:, :].bitcast(BF16)
                        S.op("pe", lambda e, pvA=pvA, nt=nt, hh=hh: e.transpose(pvA[:, 0:nt], qnb2[0:nt, hh, :], ident_b[0:nt, 0:nt]), r=["qhb2", "ident_b"], w=[("bank", bA)])
                        S.op("act", lambda e, pvA=pvA, nt=nt, h=h, c0=c0: e.activation(hT[:, h, c0:c0 + nt], pvA[:, 0:nt], AF.Copy), r=[("bank", bA)], w=[("hT", ti)])
                    bB = rbank()
                    pvB = banks[bB][:, :].bitcast(BF16)
                    S.op("pe", lambda e, pvB=pvB, nt=nt: e.transpose(pvB[:, 0:nt], qrb[0:nt, :], ident_b[0:nt, 0:nt]), r=["qhb2", "ident_b"], w=[("bank", bB)])
                    S.op("dve", lambda e, pvB=pvB, nt=nt, pr=sl * 2 + g, c0=c0: e.tensor_copy(QrT[:, pr, c0:c0 + nt], pvB[:, 0:nt]), r=[("bank", bB)], w=[("QrT", ti)])
        S.barrier(NOPS)
        esq.close()
        esa = new_scope()
        C.RB = 4
        C.bank_rr = 0
        krT_all = sb("krT_all", [128, T], BF16)
        invk_all = sb("invk_all", [128, NT * 16], F32)
        KTb = [sb("KT%d" % i, [128, 4096], BF16) for i in range(2)]
        Vb = [sb("V%d" % i, [128, 32, 128], BF16) for i in range(2)]
        Pb = [sb("Pb%d" % i, [128, 512], BF16) for i in range(4)]
        rec = sb("rec", [128, 512], F32)
        KTs = sb("KTs", [128, 17 * 128], BF16)
        Vs = sb("Vs", [128, 17, 128], BF16)
        krTs = sb("krTs", [128, 17 * 128], BF16)
        invks = sb("invks", [128, 17 * 16], F32)
        dma("sp", krT_all[0:64, :], krT_scr, [], ["krT_all"], "krT_all0")
        dma("sp", krT_all[64:128, :], krT_scr, [], ["krT_all"], "krT_all1")
        dma("sp", invk_all[:, :], invk_scr.rearrange("p j h -> p (j h)"), [], ["invk_all"], "invk_all")
        OTp, Lp = banks[4], banks[5]
        C.pj = 0

        SKEW = 3

        def attend(h, keytiles, qcol0, nq, kt_of, v_of, krt, ik_of, kb_of, diag_of, okeys):
            rp = (h % 2) * 64
            nk = len(keytiles)
            pjs = {}

            def scores(idx):
                j, c0, n = keytiles[idx]
                b = rbank()
                pt = banks[b][:, 0:n]
                ktap, ktkey = kt_of(j)
                S.op("pe", lambda e: e.matmul(pt, ktap, hT[:, h, qcol0 + c0:qcol0 + c0 + n], start=True, stop=False),
                     r=[ktkey] + okeys, w=[("bank", b)])
                S.op("pe", lambda e: e.matmul(pt, krt[rp:rp + 64, j * 128:(j + 1) * 128], QrT[rp:rp + 64, h // 2, qcol0 + c0:qcol0 + c0 + n], start=False, stop=True),
                     r=["krT_all", "krTs"], w=[("bank", b)])
                pj = C.pj = (C.pj + 1) % 4
                pjs[idx] = pj
                S.op("act", lambda e: e.activation(Pb[pj][:, 0:n], pt, AF.Exp, bias=kb_of(j), scale=ik_of(j)),
                     r=[("bank", b), "invk_all", "invks", "kb_sb"], w=[("Pb", pj)])
                if diag_of(j):
                    S.op("dve", lambda e: e.memset(Pb[pj][64:128, 0:64], 0.0), w=[("Pb", pj)])

            def pv(idx):
                j, c0, n = keytiles[idx]
                pj = pjs[idx]
                vap, vkey = v_of(j)
                S.op("pe", lambda e: e.matmul(OTp[:, c0:c0 + n], vap, Pb[pj][:, 0:n], start=(idx == 0), stop=(idx == nk - 1)),
                     r=[vkey, ("Pb", pj)], w=["OT"])
                S.op("pe", lambda e: e.matmul(Lp[:, c0:c0 + n], ones_b[:, :], Pb[pj][:, 0:n], start=(idx == 0), stop=(idx == nk - 1)),
                     r=["ones_b", ("Pb", pj)], w=["L"])

            for idx in range(nk + SKEW):
                if idx < nk:
                    scores(idx)
                if idx >= SKEW:
                    pv(idx - SKEW)
            S.op("dve", lambda e: e.reciprocal(rec[:, 0:nq], Lp[:, 0:nq]), r=["L"], w=["rec"])
            S.op("dve", lambda e: e.tensor_tensor(oT[:, h, qcol0:qcol0 + nq], OTp[:, 0:nq], rec[:, 0:nq], ALU.mult), r=["OT", "rec"], w=[("oT", h)])

        for h in range(16 if "noattn" not in KSKIP else 0):
            dma("sp", KTb[0][:, :], knT_scr[h, :, 0:4096], [], [("KT", 0)], ("KT", 0))
            dma("sp", Vb[0][:, :, :], v_scr[h, :, 0:32, :], [], [("V", 0)], ("V", 0))
            dma("sp", KTb[1][:, :], knT_scr[h, :, 4096:8192], [], [("KT", 1)], ("KT", 1))
            dma("sp", Vb[1][:, :, :], v_scr[h, :, 32:64, :], [], [("V", 1)], ("V", 1))
            for qhalf in range(2):
                kts = []
                for rr in range(4 * qhalf + 4):
                    for i in range(8):
                        kmin = max(rr, 4 * qhalf)
                        c0 = (kmin - 4 * qhalf) * 128
                        kts.append((8 * rr + i, c0, 512 - c0))
                attend(h, kts, qhalf * 512, 512,
                       lambda j: (KTb[j // 32][:, (j % 32) * 128:(j % 32 + 1) * 128], ("KT", j // 32)),
                       lambda j: (Vb[j // 32][:, j % 32, :], ("V", j // 32)),
                       krT_all,
                       lambda j, h=h: invk_all[:, j * 16 + h:j * 16 + h + 1],
                       lambda j: kb_sb[:, j:j + 1],
                       lambda j, qhalf=qhalf: (j % 8 == 7) and (j // 8 >= 4 * qhalf),
                       [("hT", k) for k in range(8)] + [("QrT", k) for k in range(8)])
        for s_ in range(2 if "noattn" not in KSKIP else 0):
            dma("sp", krTs[0:64, :], krT_s[s_], [], ["krTs"], "krTs0")
            dma("sp", krTs[64:128, :], krT_s[s_], [], ["krTs"], "krTs1")
            dma("sp", invks[:, :], invk_s[s_].rearrange("p j h -> p (j h)"), [], ["invks"], "invks")
            for h in range(16):
                dma("sp", KTs[:, :], knT_s[s_, h], [], ["KTs"], "KTs")
                dma("sp", Vs[:, :, :], v_s[s_, h], [], ["Vs"], "Vs")
                attend(h, [(j, 0, 16) for j in range(17)], 1024 + 16 * s_, 16,
                       lambda j: (KTs[:, j * 128:(j + 1) * 128], "KTs"),
                       lambda j: (Vs[:, j, :], "Vs"),
                       krTs,
                       lambda j, h=h: invks[:, j * 16 + h:j * 16 + h + 1],
                       lambda j: kb_sb[:, NT:NT + 1] if j == 16 else kb_sb[:, 7:8],
                       lambda j: False,
                       [("hT", 8), ("QrT", 8)])


        S.barrier(NOPS)
        esa.close()
        esb = new_scope()
        C.RB = 7
        C.bank_rr = 0
        alloc_wbufs()
        ogc = [sb("ogc%d" % i, [128, 544], BF16) for i in range(2)]
        tA = sb("tA", [128, 512], F32)
        tB = sb("tB", [128, 512], BF16)
        tC = sb("tC", [128, 512], BF16)
        x1 = sb("x1", [128, 5, D], F32)
        actg = sb("actg", [128, 8, 544], BF16)
        own_norm_transpose()
        for half in range(2 if "noback" not in KSKIP else 0):
            if half == 0:
                htiles = [(k, k * 128, 128, k) for k in range(4)]
                pieces = [(0, 512)]
                hc0, hn = 0, 512
            else:
                htiles = [(k, k * 128, 128, k - 4) for k in range(4, 8)] + [(8, 1024, 32, 4)]
                pieces = [(512, 512), (1024, 32)]
                hc0, hn = 512, 544
            hkeys = [("hT", t[0]) for t in htiles]
            for c in range(16):
                og = ogc[c % 2]
                for (k, c0a, nt, slot) in htiles:
                    if k < 8:
                        dma("sp", og[:, c0a - hc0:c0a - hc0 + 128], ogla_scr[k, :, c, :], [], [("ogc", c % 2)], ("ogc", c % 2, k % 2))
                    else:
                        for s_ in range(2):
                            dma("sp", og[:, 512 + s_ * 16:512 + (s_ + 1) * 16], ogla_s[s_, :, c, :], [], [("ogc", c % 2)], ("ogc", c % 2, s_))
                wog, kog = load_slab(w_in[:, C_OG + c * 128:C_OG + (c + 1) * 128], 16, 128)
                wgg, kgg = load_slab(w_in[:, C_GG + c * 128:C_GG + (c + 1) * 128], 16, 128)
                wgm, kgm = load_slab(w_in[:, C_GM + c * 128:C_GM + (c + 1) * 128], 16, 128)
                for (n0, n) in pieces:
                    proj_fm(wog, kog, 16, 0, 128, hT, hkeys, n0, n,
                            lambda pt, bk, n=n: S.op("act", lambda e: e.activation(tA[:, 0:n], pt, AF.Silu), r=[bk], w=["tA"]))
                    proj_fm(wgg, kgg, 16, 0, 128, hT, hkeys, n0, n,
                            lambda pt, bk, n=n: S.op("act", lambda e: e.activation(tB[:, 0:n], pt, AF.Sigmoid), r=[bk], w=["tB"]))
                    proj_fm(wgm, kgm, 16, 0, 128, hT, hkeys, n0, n,
                            lambda pt, bk, n=n: S.op("act", lambda e: e.activation(tC[:, 0:n], pt, AF.Sigmoid), r=[bk], w=["tC"]))
                    S.op("dve", lambda e, n=n: e.tensor_tensor(tA[:, 0:n], tA[:, 0:n], tB[:, 0:n], ALU.mult), r=["tA", "tB"], w=["tA"])
                    S.op("dve", lambda e, n=n, r0=n0 - hc0, og=og: e.tensor_tensor(tA[:, 0:n], tA[:, 0:n], og[:, r0:r0 + n], ALU.mult), r=["tA", ("ogc", c % 2)], w=["tA"])
                    S.op("dve", lambda e, n=n, n0=n0, c=c: e.tensor_tensor(tC[:, 0:n], tC[:, 0:n], oT[:, c, n0:n0 + n], ALU.mult), r=["tC", ("oT", c)], w=["tC"])
                    S.op("dve", lambda e, n=n, n0=n0, c=c: e.tensor_tensor(oT[:, c, n0:n0 + n], tA[:, 0:n], tC[:, 0:n], ALU.add), r=["tA", "tC"], w=[("oT", c)])
            for (k, c0a, nt, slot) in htiles:
                src = xs[(8 * k + 7) * 128:(8 * k + 8) * 128, :] if k < 8 else x_s
                dma("sp", x1[0:nt, slot, :], src, [], [("x1", slot)], ("x1", slot % 2))
            wtiles = [(c0a, nt, slot) for (k, c0a, nt, slot) in htiles]
            okeys = [("oT", c) for c in range(16)]
            for sl in range(8):
                wv, wkey = load_slab(w_o[:, sl * 256:(sl + 1) * 256], 16, 256)

                def ev_o(ti, slot, pt, bk, sl=sl):
                    nt = wtiles[ti][1]
                    S.op("dve", lambda e: e.tensor_tensor(x1[0:nt, slot, sl * 256:(sl + 1) * 256], x1[0:nt, slot, sl * 256:(sl + 1) * 256], pt, ALU.add),
                         r=[bk, ("x1", slot)], w=[("x1", slot)])
                for ti, (c0a, nt, slot) in enumerate(wtiles):
                    b = rbank()
                    pt = banks[b][0:nt, 0:256]
                    for kc in range(16):
                        S.op("pe", lambda e, pt=pt, kc=kc, c0a=c0a, nt=nt, wv=wv: e.matmul(pt, oT[:, kc, c0a:c0a + nt], wv[:, kc, :], start=(kc == 0), stop=(kc == 15)),
                             r=[wkey] + okeys, w=[("bank", b)])
                    ev_o(ti, slot, pt, ("bank", b))
            for (k, c0a, nt, slot) in htiles:
                c = statcol()
                ss = stat[0:nt, c:c + 1]
                S.op("act", lambda e, nt=nt, slot=slot, ss=ss: e.activation(hb[0][0:nt, :], x1[0:nt, slot, :], AF.Square, accum_out=ss), r=[("x1", slot)], w=[("stat", c), ("hb", 0)])
                act_rstd(ss, ss, float(D), [("stat", c)], [("stat", c)])
                S.op("dve", lambda e, nt=nt, slot=slot, ss=ss: e.tensor_scalar(hb[0][0:nt, :], x1[0:nt, slot, :], ss, None, ALU.mult), r=[("x1", slot), ("stat", c)], w=[("hb", 0)])
                for g4 in range(4):
                    b = rbank()
                    pv = banks[b][:, :].bitcast(BF16)
                    for j in range(4):
                        kc = g4 * 4 + j
                        S.op("pe", lambda e, pv=pv, j=j, kc=kc, nt=nt: e.transpose(pv[:, j * 128:j * 128 + nt], hb[0][0:nt, kc * 128:(kc + 1) * 128], ident_b[0:nt, 0:nt]),
                             r=[("hb", 0), "ident_b"], w=[("bank", b)])
                    for j in range(4):
                        kc = g4 * 4 + j
                        S.op("act", lambda e, pv=pv, j=j, kc=kc, nt=nt, c0a=c0a: e.activation(hT[:, kc, c0a:c0a + nt], pv[:, j * 128:j * 128 + nt], AF.Copy, scale=gffnT[:, kc:kc + 1]),
                             r=[("bank", b), "gffnT"], w=[("hT", k)])
            rtiles = [(c0a - hc0, nt, slot) for (k, c0a, nt, slot) in htiles]
            for g in range(8):
                for sl in range(4):
                    wv, wkey = load_slab(w_up[:, g * 1024 + sl * 256:g * 1024 + (sl + 1) * 256], 16, 256)
                    for cc in range(2):
                        for (n0, n) in pieces:
                            proj_fm(wv, wkey, 16, cc * 128, 128, hT, hkeys, n0, n,
                                    lambda pt, bk, r0=n0 - hc0, n=n, ch=sl * 2 + cc: (
                                        S.op("act", lambda e: e.activation(tA[:, 0:n], pt, AF.Relu), r=[bk], w=["tA"]),
                                        S.op("dve", lambda e: e.tensor_tensor(actg[:, ch, r0:r0 + n], tA[:, 0:n], tA[:, 0:n], ALU.mult), r=["tA"], w=[("actg", ch)])))
                akeys = [("actg", ch) for ch in range(8)]
                for sl in range(4):
                    wv, wkey = load_slab(w_dn[g * 1024:(g + 1) * 1024, sl * 512:(sl + 1) * 512], 8, 512)
                    for ti, (c0r, nt, slot) in enumerate(rtiles):
                        b = rbank()
                        pt = banks[b][0:nt, 0:512]
                        for kc in range(8):
                            S.op("pe", lambda e, pt=pt, kc=kc, c0r=c0r, nt=nt, wv=wv: e.matmul(pt, actg[:, kc, c0r:c0r + nt], wv[:, kc, :], start=(kc == 0), stop=(kc == 7)),
                                 r=[wkey] + akeys, w=[("bank", b)])
                        S.op("dve", lambda e, pt=pt, nt=nt, slot=slot, sl=sl: e.tensor_tensor(x1[0:nt, slot, sl * 512:(sl + 1) * 512], x1[0:nt, slot, sl * 512:(sl + 1) * 512], pt, ALU.add),
                             r=[("bank", b), ("x1", slot)], w=[("x1", slot)])
            for (k, c0a, nt, slot) in htiles:
                dst = y_p[k * 128:(k + 1) * 128, :] if k < 8 else y_s
                dma("act", dst, x1[0:nt, slot, :], [("x1", slot)], [], ("x1_o", slot % 2))

    if "bar" in KSKIP:
        S.barrier(NOPS)
    if "nodbg" in KSKIP:
        pass
    elif "dbgA" in KSKIP:
        dma("sp", hb[0][:, 0:128], xs[0:128, 0:64].bitcast(BF16), [], [("hb", 0)], "dbg0")
        dma("sp", xt[0][:, 0:16], xs[0:128, 0:16], [], [("xt", 0)], "dbg1")
    elif "dbgsmall" in KSKIP:
        dma("sp", hb[0][0:64, 128:256], krT_scr[:, 0:128], ["scr_krT"], [("hb", 0)], "dbg0")
        dma("sp", xt[0][:, 0:16], invk_scr[:, 0, :], ["scr_invk"], [("xt", 0)], "dbg1")
    elif stage < 3:
        dma("sp", hb[0][:, 0:128], knT_scr[0, :, 0:128], ["scr_knT"], [("hb", 0)], "dbg0")
        dma("sp", hb[0][0:64, 128:256], krT_scr[:, 0:128], ["scr_krT"], [("hb", 0)], "dbg0")
        dma("sp", hb[0][:, 256:384], v_scr[0, :, 0, :], ["scr_v"], [("hb", 0)], "dbg0")
        dma("sp", xt[0][:, 0:16], invk_scr[:, 0, :], ["scr_invk"], [("xt", 0)], "dbg1")
        dma("sp", hb[0][:, 384:512], ogla_scr[0, :, 0, :], [], [("hb", 0)], "dbg0")
        if stage >= 2:
            dma("sp", hb[0][:, 0:128], knT_s[0, 0, :, 0:128], ["scr_knT"], [("hb", 0)], "dbg0")
            dma("sp", hb[0][0:64, 128:256], krT_s[0, :, 0:128], ["scr_krT"], [("hb", 0)], "dbg0")
            dma("sp", hb[0][:, 256:384], v_s[0, 0, :, 0, :], ["scr_v"], [("hb", 0)], "dbg0")
            dma("sp", xt[0][:, 0:16], invk_s[0, :, 0, :], ["scr_invk"], [("xt", 0)], "dbg1")
            dma("sp", hb[0][:, 384:400], ogla_s[0, :, 0, :], [], [("hb", 0)], "dbg0")

    C.S, C.nc, C.es = S, nc, es
    C.__dict__.update(dict(locals()))
    return C


def finish_program(C):
    from contextlib import ExitStack
    nc, S = C.nc, C.S
    for sc in reversed(getattr(C, "scopes", [])):
        sc.close()
    C.cur = C.es
    S.assign()
    keys = set()
    for e in S.ENGS:
        for I in S.ins[e]:
            if I.signal:
                keys.add(I.semkey)
    sems = {k: C.es.enter_context(nc.semaphore("s_%d" % i)) for i, k in enumerate(sorted(keys, key=str))}
    with nc.Block() as block:
        @block.tensor
        def _(e):
            S.emit_engine("pe", e, sems)

        @block.vector
        def _(e):
            S.emit_engine("dve", e, sems)

        @block.scalar
        def _(e):
            S.emit_engine("act", e, sems)

        @block.gpsimd
        def _(e):
            S.emit_engine("pool", e, sems)

        @block.sync
        def _(e):
            S.emit_engine("sp", e, sems)
    C.es.close()
    return nc, len(sems)


def _rope_table(pos):
    half = 32
    freqs = np.power(np.float32(10000.0), -np.arange(half, dtype=np.float32) / np.float32(half)).astype(np.float32)
    ang = pos.astype(np.float32)[:, None] * freqs[None, :]
    return np.concatenate([np.cos(ang), np.sin(ang)], axis=1).astype(np.float32)


def make_in_maps(inp):
    f = lambda k: np.ascontiguousarray(inp[k], dtype=np.float32)
    xp = f("x_prompt")[0]
    xsmp = f("x_sample")
    idn = np.eye(128, dtype=np.float32)
    sidx, tidx = np.meshgrid(np.arange(128), np.arange(128), indexing="ij")
    tri = (sidx <= tidx).astype(np.float32)
    aft = (sidx > tidx).astype(np.float32)
    consts = np.concatenate([idn, tri, aft], axis=1)
    shared = dict(
        consts=consts, w_in=f("w_in")[0], w_uq=f("mla_w_uq")[0].reshape(512, 3072), w_ukv=f("mla_w_ukv")[0].reshape(512, 4096),
        w_a2=f("gla_w_a2")[0], b_a=f("gla_b_a"), w_o=f("w_o")[0], w_up=f("ffn_w_up")[0], w_dn=f("ffn_w_down")[0],
        g_mix=f("norm_mix_g"), g_ffn=f("norm_ffn_g"), g_q=f("mla_q_norm_g"), g_kv=f("mla_kv_norm_g"), g_gla=f("gla_norm_g"),
        gq_n=f("mla_q_gain_nope"), gk_n=f("mla_k_gain_nope"), gq_r=f("mla_q_gain_rope"), gk_r=f("mla_k_gain_rope"))
    maps = []
    for c in range(NCORES):
        xs = np.zeros((NT * 128, D), np.float32)
        pos = np.zeros((NT * 128,), np.float32)
        kb = np.zeros((128, NT + 1), np.float32)
        for j in range(NT):
            t = c - 7 + j
            if t >= 0:
                xs[j * 128:(j + 1) * 128] = xp[t * 128:(t + 1) * 128]
                pos[j * 128:(j + 1) * 128] = t * 128 + np.arange(128)
            else:
                kb[:, j] = NEG
        kb[16:, NT] = NEG
        m = dict(shared)
        m.update(xs=xs, ropet=_rope_table(pos), kbias=kb,
                 x_s=np.ascontiguousarray(xsmp[2 * c:2 * c + 2].reshape(32, D)),
                 ropes=_rope_table(np.tile(2048.0 + np.arange(16, dtype=np.float32), 2)),
                 cckv=np.ascontiguousarray(f("cache_mla_ckv")[0, 2 * c:2 * c + 2]),
                 ckr=np.ascontiguousarray(f("cache_mla_krope")[0, 2 * c:2 * c + 2]),
                 st_in=np.ascontiguousarray(f("state_gla")[0, 2 * c:2 * c + 2]))
        maps.append(m)
    return maps


def assemble(results):
    y_p = np.zeros((1, T, D), np.float32)
    ckv_p = np.zeros((1, 1, T, 512), np.float32)
    kr_p = np.zeros((1, 1, T, 64), np.float32)
    y_s = np.zeros((16, 16, D), np.float32)
    ckv_s = np.zeros((1, 16, 16, 512), np.float32)
    kr_s = np.zeros((1, 16, 16, 64), np.float32)
    st_s = np.zeros((1, 16, 4, 256, 512), np.float32)
    class _Z(dict):
        def __missing__(self, k):
            shp = {"y_p": (1024, D), "ckv_p": (1024, 512), "kr_p": (1024, 64), "st_p": (4, 256, 512), "y_s": (32, D),
                   "ckv_s": (32, 512), "kr_s": (32, 64), "st_s": (2, 4, 256, 512)}[k]
            return np.zeros(shp, np.float32)
    results = [_Z(r) for r in results]
    for c, r in enumerate(results):
        for k in range(8):
            t = c + 8 * k
            y_p[0, t * 128:(t + 1) * 128] = r["y_p"][k * 128:(k + 1) * 128]
            ckv_p[0, 0, t * 128:(t + 1) * 128] = r["ckv_p"][k * 128:(k + 1) * 128]
            kr_p[0, 0, t * 128:(t + 1) * 128] = r["kr_p"][k * 128:(k + 1) * 128]
        y_s[2 * c:2 * c + 2] = r["y_s"].reshape(2, 16, D)
        ckv_s[0, 2 * c:2 * c + 2] = r["ckv_s"].reshape(2, 16, 512)
        kr_s[0, 2 * c:2 * c + 2] = r["kr_s"].reshape(2, 16, 64)
        st_s[0, 2 * c:2 * c + 2] = r["st_s"]
    st_p = np.ascontiguousarray(results[7]["st_p"], dtype=np.float32).reshape(1, 1, 4, 256, 512)
    return (y_p, y_s, ckv_p, kr_p, st_p, ckv_s, kr_s, st_s)


def kernel(**inputs):
    C = build_program()
    nc, _ = finish_program(C)
    res = run_bass_kernel_spmd(nc, make_in_maps(inputs), core_ids=list(range(NCORES)))
    return assemble(res.results)
```
